# Optimizing a Trainium2 kernel written in Bass

```python
import jax, jax.numpy as jnp
from jax import lax
import numpy as np

D_MODEL = 1024
BATCH = 8
SEQ = 2048
DEPTH = 2
DEC_BATCH = 128
DEC_SEQ = 8
PAST_LEN = 16384
PAGE_SIZE = 128

N_EVEN = (DEPTH + 1) // 2
N_ODD = DEPTH // 2
D_A = D_MODEL
CONV_W = 31
H_B = 8
DK_B = 128
DV_B = 128
D_B = H_B * DV_B
CHUNK_B = 64
D_C = 2 * D_MODEL
H_C = 8
DG_C = D_C // H_C
CHUNK_C = 128
EPS = 1e-6
HK_B = H_B * DK_B
D_IN_AB = 3 * D_A + 2 * HK_B + 2 * D_B
SPLIT_AB = [D_A, 2 * D_A, 3 * D_A, 3 * D_A + HK_B, 3 * D_A + 2 * HK_B, 3 * D_A + 2 * HK_B + D_B]

kernel_name = 'hybrid_conformerconv_hgrn2_gmlp_decode_step'


def _rmsnorm(x, g):
    xf = x.astype(jnp.float32)
    y = xf * lax.rsqrt(jnp.mean(xf * xf, axis=-1, keepdims=True) + EPS)
    return (y * g.astype(jnp.float32)).astype(x.dtype)


def _layernorm(x, g, b):
    xf = x.astype(jnp.float32)
    xc = xf - jnp.mean(xf, axis=-1, keepdims=True)
    y = xc * lax.rsqrt(jnp.mean(xc * xc, axis=-1, keepdims=True) + EPS)
    return (y * g.astype(jnp.float32) + b.astype(jnp.float32)).astype(x.dtype)


def _conformer_conv(a_val, a_glu, buf, conv_w, conv_b, ln_g, ln_b):
    a = a_val * jax.nn.sigmoid(a_glu)
    xp = jnp.concatenate([buf.astype(a.dtype), a], axis=1)
    y = lax.conv_general_dilated(xp, conv_w[:, None, :].astype(a.dtype), window_strides=(1,),
                                 padding='VALID', dimension_numbers=('NWC', 'WIO', 'NWC'),
                                 feature_group_count=D_A)
    y = jax.nn.silu(_layernorm(y + conv_b.astype(a.dtype), ln_g, ln_b))
    return y, xp[:, xp.shape[1] - (CONV_W - 1):]


def _hgrn2(q, f_pre, i_in, s0, lb):
    B, L = q.shape[0], q.shape[1]
    c = min(CHUNK_B, L)
    n = -(-L // c)
    pad = n * c - L
    f = lb + (1.0 - lb) * jax.nn.sigmoid(f_pre.astype(jnp.float32))
    logf = jnp.log(f)
    k = 1.0 - f
    qf = q.astype(jnp.float32)
    vf = i_in.astype(jnp.float32)
    padw = ((0, 0), (0, pad), (0, 0), (0, 0))
    qf, logf, k, vf = [jnp.pad(t, padw).reshape(B, n, c, H_B, t.shape[-1]) for t in (qf, logf, k, vf)]
    bcum = jnp.cumsum(logf, axis=2)
    b_mid = bcum[:, :, c // 2:c // 2 + 1]
    b_end = bcum[:, :, c - 1:c]
    scores = jnp.einsum('bnthd,bnshd->bnhts', qf * jnp.exp(bcum - b_mid), k * jnp.exp(b_mid - bcum))
    causal = jnp.tril(jnp.ones((c, c), dtype=bool))
    scores = jnp.where(causal, scores, 0.0)
    o_intra = jnp.einsum('bnhts,bnshv->bnthv', scores, vf)
    ds = jnp.einsum('bnshd,bnshv->bnhdv', k * jnp.exp(b_end - bcum), vf)
    decay = jnp.exp(b_end[:, :, 0])

    def step(s, inp):
        dec, d = inp
        return dec[..., None] * s + d, s

    s_fin, s_prev = lax.scan(step, s0.astype(jnp.float32),
                             (jnp.moveaxis(decay, 1, 0), jnp.moveaxis(ds, 1, 0)))
    s_prev = jnp.moveaxis(s_prev, 0, 1)
    o_inter = jnp.einsum('bnthd,bnhdv->bnthv', qf * jnp.exp(bcum), s_prev)
    o = (o_intra + o_inter).reshape(B, n * c, H_B, DV_B)[:, :L]
    return o, s_fin


def _even_layer(x, buf, s0, g_norm, w_in, conv_w, conv_b, ln_g, ln_b, lb, o_g, w_out):
    B, L, _ = x.shape
    z = _rmsnorm(x, g_norm) @ w_in
    a_val, a_glu, a_gate, q, f_pre, i_in, b_gate = jnp.split(z, SPLIT_AB, axis=-1)
    a_out, new_buf = _conformer_conv(a_val, a_glu, buf, conv_w, conv_b, ln_g, ln_b)
    a_out = a_out * jax.nn.silu(a_gate)
    o, s_new = _hgrn2(q.reshape(B, L, H_B, DK_B), f_pre.reshape(B, L, H_B, DK_B),
                      jax.nn.silu(i_in).reshape(B, L, H_B, DV_B), s0, lb)
    o = _rmsnorm(o, o_g).reshape(B, L, D_B).astype(x.dtype) * jax.nn.silu(b_gate)
    y = jnp.concatenate([a_out, o], axis=-1) @ w_out
    return x + y, new_buf, s_new


def _odd_layer(x, g_norm, w_in, ln_g, ln_b, w_s, b_s, w_out):
    B, L, _ = x.shape
    z = _rmsnorm(x, g_norm) @ w_in
    uv = jax.nn.gelu(z[..., :2 * D_C])
    gate = z[..., 2 * D_C:]
    u = uv[..., :D_C]
    v = _layernorm(uv[..., D_C:], ln_g, ln_b)
    n = -(-L // CHUNK_C)
    pad = n * CHUNK_C - L
    vp = jnp.pad(v, ((0, 0), (0, pad), (0, 0))).reshape(B, n, CHUNK_C, H_C, DG_C)
    ws = jnp.where(jnp.tril(jnp.ones((CHUNK_C, CHUNK_C), dtype=bool)), w_s, 0.0)
    mix = jnp.einsum('hts,bnshd->bnthd', ws.astype(vp.dtype), vp) + b_s.T[None, None, :, :, None]
    mix = mix.reshape(B, n * CHUNK_C, D_C)[:, :L]
    y = (u * mix * jax.nn.silu(gate)) @ w_out
    start = ((L - 1) // CHUNK_C) * CHUNK_C
    return x + y, v[:, start:]


def _trunk(x, bufs, states, norm_ab, w_in_ab, conv_w, conv_b, ln_a_g, ln_a_b, lb_logits, onorm_b,
           w_out_ab, norm_c, w_in_c, ln_c_g, ln_c_b, w_s, b_s, w_out_c, final_norm):
    lb_all = jnp.cumsum(jax.nn.softmax(lb_logits.astype(jnp.float32), axis=0), axis=0)
    new_bufs, new_states, new_v = [], [], []
    for l in range(DEPTH):
        j = l // 2
        if l % 2 == 0:
            x, nb, ns = _even_layer(x, bufs[j], states[j], norm_ab[j], w_in_ab[j], conv_w[j], conv_b[j],
                                    ln_a_g[j], ln_a_b[j], lb_all[j].reshape(H_B, DK_B), onorm_b[j],
                                    w_out_ab[j])
            new_bufs.append(nb)
            new_states.append(ns.astype(x.dtype))
        else:
            x, nv = _odd_layer(x, norm_c[j], w_in_c[j], ln_c_g[j], ln_c_b[j], w_s[j], b_s[j], w_out_c[j])
            new_v.append(nv)
    return _rmsnorm(x, final_norm), jnp.stack(new_bufs), jnp.stack(new_states), jnp.stack(new_v)


def setup_inputs(seed: int = 0) -> dict:
    key = jax.random.key(seed)
    ks = jax.random.split(key, 24)
    nrm = lambda k, s: jax.random.normal(k, s, jnp.float32)
    return {
        'x_prompt': nrm(ks[0], (BATCH, SEQ, D_MODEL)),
        'x_sample': nrm(ks[1], (DEC_BATCH, DEC_SEQ, D_MODEL)),
        'state_conv': 0.5 * nrm(ks[2], (N_EVEN, DEC_BATCH, CONV_W - 1, D_A)),
        'state_hgrn': 0.5 * nrm(ks[3], (N_EVEN, DEC_BATCH, H_B, DK_B, DV_B)),
        'norm_ab': 1.0 + 0.02 * nrm(ks[4], (N_EVEN, D_MODEL)),
        'w_in_ab': nrm(ks[5], (N_EVEN, D_MODEL, D_IN_AB)) * D_MODEL ** -0.5,
        'conv_w': nrm(ks[6], (N_EVEN, CONV_W, D_A)) * CONV_W ** -0.5,
        'conv_b': 0.02 * nrm(ks[7], (N_EVEN, D_A)),
        'ln_a_g': 1.0 + 0.02 * nrm(ks[8], (N_EVEN, D_A)),
        'ln_a_b': 0.02 * nrm(ks[9], (N_EVEN, D_A)),
        'lb_logits': 0.1 * nrm(ks[10], (N_EVEN + 1, HK_B)),
        'onorm_b': 1.0 + 0.02 * nrm(ks[11], (N_EVEN, H_B, DV_B)),
        'w_out_ab': nrm(ks[12], (N_EVEN, D_A + D_B, D_MODEL)) * (D_A + D_B) ** -0.5,
        'norm_c': 1.0 + 0.02 * nrm(ks[13], (N_ODD, D_MODEL)),
        'w_in_c': nrm(ks[14], (N_ODD, D_MODEL, 3 * D_C)) * D_MODEL ** -0.5,
        'ln_c_g': 1.0 + 0.02 * nrm(ks[15], (N_ODD, D_C)),
        'ln_c_b': 0.02 * nrm(ks[16], (N_ODD, D_C)),
        'w_s': nrm(ks[17], (N_ODD, H_C, CHUNK_C, CHUNK_C)) * CHUNK_C ** -0.5,
        'b_s': 1.0 + 0.1 * nrm(ks[18], (N_ODD, H_C, CHUNK_C)),
        'w_out_c': nrm(ks[19], (N_ODD, D_C, D_MODEL)) * D_C ** -0.5,
        'final_norm': 1.0 + 0.02 * nrm(ks[20], (D_MODEL,)),
    }


def reference(x_prompt, x_sample, state_conv, state_hgrn, norm_ab, w_in_ab, conv_w, conv_b, ln_a_g,
              ln_a_b, lb_logits, onorm_b, w_out_ab, norm_c, w_in_c, ln_c_g, ln_c_b, w_s, b_s, w_out_c,
              final_norm):
    weights = (norm_ab, w_in_ab, conv_w, conv_b, ln_a_g, ln_a_b, lb_logits, onorm_b, w_out_ab,
               norm_c, w_in_c, ln_c_g, ln_c_b, w_s, b_s, w_out_c, final_norm)
    bp = x_prompt.shape[0]
    bufs0 = jnp.zeros((N_EVEN, bp, CONV_W - 1, D_A), x_prompt.dtype)
    states0 = jnp.zeros((N_EVEN, bp, H_B, DK_B, DV_B), jnp.float32)
    y_prompt, conv_prompt, hgrn_prompt, gmlp_v_prompt = _trunk(x_prompt, bufs0, states0, *weights)
    y_sample, conv_sample, hgrn_sample, gmlp_v_sample = _trunk(x_sample, state_conv, state_hgrn, *weights)
    return (y_prompt, y_sample, conv_prompt, hgrn_prompt, gmlp_v_prompt, conv_sample, hgrn_sample, gmlp_v_sample)
```

```python
import contextlib
import numpy as np
import concourse.bass as bass
import concourse.mybir as mybir
from concourse.bass_utils import run_bass_kernel_spmd

F32 = mybir.dt.float32
BF16 = mybir.dt.bfloat16
AF = mybir.ActivationFunctionType
ALU = mybir.AluOpType
AX = mybir.AxisListType

NCORES = 8
D = 1024
SEQ = 2048
NSUB = 2
NARENA = 2
LIST_SCHED = True
ACT_SWITCH_NS = 0.0
ACT_SWITCH_COST = 1300.0
DG_SPLIT = 20
VLN_SPLIT = 640
CAT_ENG = "dve"
W_QUEUE = "sp"
L0_AB = False
CONV_PE_TAPS = 31
SCHED_WINDOW = 100
SAMPLE_ALONE = True
NSLOT = 3
EPS = 1e-6

C_ID, C_TRIL, C_UCP, C_MKP, C_UXP = 0, 128, 256, 384, 512
C_UCS, C_MKS, C_UXS, C_CMS, C_BD, C_R = 520, 648, 776, 824, 840, 968
NCST = 1096


def make_consts():
    c = np.zeros((128, NCST), np.float32)
    i = np.arange(128)
    c[:, C_ID:C_ID + 128] = np.eye(128)
    c[:, C_TRIL:C_TRIL + 128] = (i[None, :] <= i[:, None])

    def fill(clen, ucol, mcol, xcol):
        ch = i // clen
        mid = ch * clen + clen // 2
        same = ch[:, None] == ch[None, :]
        c[:, ucol:ucol + 128] = same * ((i[:, None] <= i[None, :]).astype(np.float32)
                                        - (i[:, None] <= mid[None, :]).astype(np.float32))
        c[:, mcol:mcol + 128] = same * (i[:, None] <= i[None, :])
        for cc in range(128 // clen):
            inc = ch == cc
            c[:, xcol + 3 * cc + 0] = inc * (i <= mid)
            c[:, xcol + 3 * cc + 1] = inc * (i > mid)
            c[:, xcol + 3 * cc + 2] = inc
    fill(64, C_UCP, C_MKP, C_UXP)
    fill(8, C_UCS, C_MKS, C_UXS)
    for cc in range(16):
        c[:, C_CMS + cc] = (i // 8 == cc)
    c[:, C_BD:C_BD + 128] = (i[:, None] // 8 == i[None, :] // 8)
    for a in range(8):
        c[a, C_R:C_R + 128] = (i % 8 == a)
    return c


class Buf:
    __slots__ = ("name", "writer", "readers")

    def __init__(self, name):
        self.name = name
        self.writer = None
        self.readers = []


class Op:
    __slots__ = ("eng", "fn", "deps", "ticket", "has_dep", "dma_key", "idx", "cost", "nbytes", "fin", "seq", "wn", "tbl")

    def __init__(self, eng, fn, deps, dma_key=None):
        self.cost = 100.0
        self.nbytes = 0
        self.tbl = 0
        self.fin = None
        self.seq = 0
        self.eng = eng
        self.fn = fn
        self.deps = deps
        self.ticket = None
        self.has_dep = False
        self.dma_key = dma_key
        self.idx = None


class Prog:
    ENGS = ("pe", "act", "dve", "pool", "sp")

    def __init__(self):
        self.ops = {e: [] for e in self.ENGS}
        self.dma_keys = {}
        self.outs = []

    def op(self, eng, fn, reads=(), writes=(), dma_key=None, is_output=False, cost=100.0, nbytes=0):
        d = set()
        for b in reads:
            if b.writer is not None:
                d.add(b.writer)
        for b in writes:
            if b.writer is not None:
                d.add(b.writer)
            d.update(b.readers)
        o = Op(eng, fn, d, dma_key)
        o.cost = cost
        o.nbytes = nbytes
        o.wn = ",".join(b.name for b in writes) + "<" + ",".join(b.name for b in reads)
        self.nops = getattr(self, "nops", 0) + 1
        o.seq = self.nops
        o.idx = len(self.ops[eng])
        self.ops[eng].append(o)
        for b in reads:
            b.readers.append(o)
        for b in writes:
            b.writer = o
            b.readers = []
        if dma_key is not None:
            self.dma_keys.setdefault(dma_key, eng)
            assert self.dma_keys[dma_key] == eng
        if is_output:
            self.outs.append(o)
        return o

    def schedule(self, window=40):
        lists = {e: list(self.ops[e]) for e in self.ENGS}
        out = {e: [] for e in self.ENGS}
        free = {e: 0.0 for e in self.ENGS}
        hbm_free = 0.0
        cur_tbl = [0]
        remaining = sum(len(v) for v in lists.values())
        while remaining:
            best = None
            for e in self.ENGS:
                L = lists[e]
                for i in range(min(window, len(L))):
                    o = L[i]
                    rdy = 0.0
                    ok = True
                    for dd in o.deps:
                        if dd.fin is None:
                            ok = False
                            break
                        lat = dd.fin if dd.eng == e and dd.dma_key is None else dd.fin + 70.0
                        if lat > rdy:
                            rdy = lat
                    if not ok:
                        continue
                    start = max(free[e], rdy)
                    sw = e == "act" and o.tbl and o.tbl != cur_tbl[0]
                    pen = ACT_SWITCH_NS if sw else 0.0
                    key = (start + pen, o.seq)
                    if best is None or key < best[0]:
                        best = (key, e, i, o, start + (ACT_SWITCH_COST if sw else 0.0))
                    if start <= free[e] and not pen:
                        break
            _, e, i, o, start = best
            lists[e].pop(i)
            out[e].append(o)
            if e == "act" and o.tbl:
                cur_tbl[0] = o.tbl
            if o.dma_key is not None:
                free[e] = start + o.cost
                t0 = max(free[e], hbm_free)
                hbm_free = t0 + o.nbytes / 330.0
                o.fin = hbm_free + 1800.0
            else:
                free[e] = start + o.cost
                o.fin = free[e]
            remaining -= 1
        for e in self.ENGS:
            for k, o in enumerate(out[e]):
                o.idx = k
            self.ops[e] = out[e]
        self.est_ns = max(o.fin for e in self.ENGS for o in self.ops[e])

    def emit(self, nc, stack):
        def skip(dd, o):
            return dd.dma_key is None and o.dma_key is None and dd.eng == "pe" and o.eng == "pe"
        def needed(o):
            last = {}
            out = []
            for dd in o.deps:
                if skip(dd, o):
                    continue
                if dd.dma_key is not None:
                    out.append(dd)
                elif dd.eng not in last or dd.idx > last[dd.eng].idx:
                    last[dd.eng] = dd
            return out + list(last.values())
        for e in self.ENGS:
            for k, o in enumerate(self.ops[e]):
                o.idx = k
        for e in self.ENGS:
            for o in self.ops[e]:
                for dd in needed(o):
                    dd.has_dep = True
        sems = {e: stack.enter_context(nc.semaphore("s_" + e)) for e in ("pe", "act", "dve", "pool")}
        dsems = {k: stack.enter_context(nc.semaphore("d_" + k)) for k in self.dma_keys}
        dcnt = {k: 0 for k in self.dma_keys}
        for e in self.ENGS:
            cnt = 0
            for o in self.ops[e]:
                if o.dma_key is not None:
                    dcnt[o.dma_key] += 16
                    o.ticket = dcnt[o.dma_key]
                elif o.has_dep:
                    cnt += 1
                    o.ticket = cnt
        fw = {}
        for o in self.outs:
            fw[o.dma_key] = max(fw.get(o.dma_key, 0), o.ticket)
        block = stack.enter_context(nc.Block())
        handles = {"pe": "tensor", "act": "scalar", "dve": "vector", "pool": "gpsimd", "sp": "sync"}

        def make_body(e):
            def body(eng):
                waited = {}
                for o in self.ops[e]:
                    for dd in sorted(needed(o), key=lambda z: (z.eng, z.idx)):
                        if dd.dma_key is not None:
                            key, sem = ("d", dd.dma_key), dsems[dd.dma_key]
                        else:
                            key, sem = ("e", dd.eng), sems[dd.eng]
                        if waited.get(key, 0) >= dd.ticket:
                            continue
                        eng.wait_ge(sem, dd.ticket)
                        waited[key] = dd.ticket
                    ins = o.fn(eng)
                    if o.dma_key is not None:
                        ins.then_inc(dsems[o.dma_key], 16)
                    elif o.has_dep:
                        ins.then_inc(sems[e], 1)
                if e == "sp":
                    for k, t in fw.items():
                        eng.wait_ge(dsems[k], t)
            return body

        for e in self.ENGS:
            getattr(block, handles[e])(make_body(e))


class T:
    def __init__(self, t, name, buf=None):
        self.t = t
        self.b = buf if buf is not None else Buf(name)


def build_program():
    nc = bass.Bass("TRN2", target_bir_lowering=False)
    din = lambda n, s: nc.dram_tensor(n, list(s), F32, kind="ExternalInput").ap()
    dout = lambda n, s: nc.dram_tensor(n, list(s), F32, kind="ExternalOutput").ap()
    xp = din("xp", (SEQ, D)); xsm = din("xsm", (128, D))
    sconv = din("sconv", (16 * 30, D)); shg = din("shg", (16, 8, 128, 128))
    norm_ab = din("norm_ab", (1, D)); w_in_ab = din("w_in_ab", (D, 7168))
    conv_w = din("conv_w", (31, D)); conv_b = din("conv_b", (1, D))
    ln_a_g = din("ln_a_g", (1, D)); ln_a_b = din("ln_a_b", (1, D))
    lb_logits = din("lb_logits", (2, D)); onorm_b = din("onorm_b", (8, 128))
    w_out_ab = din("w_out_ab", (2048, D)); norm_c = din("norm_c", (1, D))
    w_in_c = din("w_in_c", (D, 6144)); ln_c_g = din("ln_c_g", (1, 2048)); ln_c_b = din("ln_c_b", (1, 2048))
    w_s = din("w_s", (8, 128, 128)); b_s = din("b_s", (8, 128))
    w_out_c = din("w_out_c", (2048, D)); final_norm = din("final_norm", (1, D))
    cst_d = din("cst", (128, NCST))
    y_p = dout("y_p", (SEQ, D)); y_s = dout("y_s", (128, D))
    conv_p = dout("conv_p", (30, D)); hg_p = dout("hg_p", (8, 128, 128)); gv_p = dout("gv_p", (128, 2048))
    conv_s = dout("conv_s", (16, 30, D)); hg_s = dout("hg_s", (16, 8, 128, 128)); gv_s = dout("gv_s", (128, 2048))

    P = Prog()
    with contextlib.ExitStack() as st:
        def sb(name, cols, dt=F32, parts=128):
            return T(st.enter_context(nc.sbuf_tensor("sb_" + name, [parts, cols], dt)), name)

        def ps(name, cols, dt=F32):
            return T(st.enter_context(nc.psum_tensor("ps_" + name, [128, cols], dt)), name)

        PZ = [ps("PZ0", 512), ps("PZ1", 512)]
        PT = ps("PT", 1024, BF16)
        PW = ps("PW", 1024)
        PO = ps("PO", 1024)
        PX = ps("PX", 512)
        cst = sb("cst", NCST)
        identb = sb("identb", 128, BF16)
        onesm = sb("onesm", 128, BF16)
        onesb = sb("onesb", 128, BF16)
        onesmf = sb("onesmf", 128)
        mh = sb("mh", 1)
        lb_bc = sb("lb_bc", D); oml_bc = sb("oml_bc", D); fin_bc = sb("fin_bc", D)
        lncg_bc = sb("lncg_bc", 2048); lncb_bc = sb("lncb_bc", 2048)
        stg = sb("stg", D + 256, F32, parts=36)
        cpar = sb("cpar", 8 * 36)
        stg2 = sb("stg2", 128, F32, parts=8)
        gon = sb("gon", 8)
        bsT = sb("bsT", 16)
        wsT = sb("wsT", 1024, BF16); wsS = sb("wsS", 1024, BF16)
        wslot = [sb("wslot%d" % i, 8 * 512, BF16) for i in range(NSLOT)]
        xs = [sb("xs%d" % j, D) for j in range(NARENA)]
        xnb = [sb("xnb%d" % j, D, BF16) for j in range(NARENA)]
        xnT = [sb("xnT%d" % j, D, BF16) for j in range(NARENA)]
        rt = [sb("rt%d" % j, 16) for j in range(NARENA)]
        FE = [sb("FE%d" % j, 2048) for j in range(NARENA)]
        kk = xnb
        KQ = [sb("KQ%d" % j, 2048, BF16) for j in range(NARENA)]
        VS = [sb("VS%d" % j, 2048, BF16) for j in range(NARENA)]
        A = [sb("A%d" % j, D) for j in range(NARENA)]
        SMG = [sb("SMG%d" % j, 2048, BF16) for j in range(NARENA)]
        QKT = [sb("QKT%d" % j, 2048, BF16) for j in range(NARENA)]
        cat = [sb("cat%d" % j, 2048, BF16) for j in range(NARENA)]
        scal = [sb("scal%d" % j, 8 * 48) for j in range(NARENA)]
        TMP1 = sb("TMP1", D); TMP2 = sb("TMP2", D)
        Sst = sb("Sst", D); Sp = sb("Sp", D, BF16); osq = sb("osq", D, BF16)
        GS = Sp; VM = osq; Xs = TMP2; bs1 = stg
        SA = (SEQ // 128) % NARENA
        Sld = [A[1], xs[1]]
        LMAX = 128 * NSUB
        yb = sb("yb", 8 * LMAX); ysq = sb("ysq", 8 * LMAX, BF16)
        ybB = [Buf("yb%d" % c) for c in range(8)]
        yb2B = [Buf("yb2_%d" % c) for c in range(8)]
        dgb = [sb("dgb%d" % i, 31 * 128, BF16) for i in range(2)]
        dgB = [[Buf("dgb%d_%d" % (i, q)) for q in range(2)] for i in range(2)]
        MEAN = sb("MEAN", LMAX); VAR = sb("VAR", LMAX); RSTD = sb("RSTD", LMAX)
        CBW = max(30 + LMAX, 16 * 38)
        cb = sb("cb", 8 * CBW, BF16)
        def fsz(ap):
            n = 1
            for s_ in ap.shape[1:]:
                n *= int(s_)
            return n

        def esz(ap):
            return 2 if ap.dtype == BF16 else 4

        def dma(eng, out, in_, reads, writes, key, is_output=False):
            nb = int(out.shape[0]) * fsz(out) * max(esz(out), esz(in_))
            return P.op(eng, lambda e: e.dma_start(out=out, in_=in_), reads, writes, dma_key=key, is_output=is_output,
                        cost=1100.0 if eng == "pool" else 80.0, nbytes=nb)

        ACT_TBL = {AF.Sigmoid: 1, AF.Silu: 2, AF.Ln: 3, AF.Exp: 3, AF.Gelu_apprx_tanh: 4}

        def act(out, in_, func, reads, writes, **kw):
            o = P.op("act", lambda e: e.activation(out=out, in_=in_, func=func, **kw), reads, writes,
                     cost=(fsz(out) + 260) / 1.2)
            o.tbl = ACT_TBL.get(func, 0)
            return o

        def vcost(eng, out):
            n = fsz(out)
            return n * 2.3 + 120 if eng == "pool" else n / 0.96 + 90

        def tt(eng, out, in0, in1, op, reads, writes):
            return P.op(eng, lambda e: e.tensor_tensor(out=out, in0=in0, in1=in1, op=op), reads, writes, cost=vcost(eng, out))

        def ts(eng, out, in0, s1, s2, op0, op1, reads, writes):
            return P.op(eng, lambda e: e.tensor_scalar(out=out, in0=in0, scalar1=s1, scalar2=s2, op0=op0, op1=op1),
                        reads, writes, cost=vcost(eng, out))

        def tsm(eng, out, in0, s1, reads, writes):
            return P.op(eng, lambda e: e.tensor_scalar_mul(out=out, in0=in0, scalar1=s1), reads, writes, cost=vcost(eng, out))

        def cp(eng, out, in_, reads, writes):
            if eng == "act":
                return P.op(eng, lambda e: e.activation(out=out, in_=in_, func=AF.Copy), reads, writes,
                            cost=(fsz(out) + 260) / 1.2)
            return P.op(eng, lambda e: e.tensor_copy(out=out, in_=in_), reads, writes, cost=vcost(eng, out))

        def mm(out, lhsT, rhs, start, stop, reads, writes):
            c = max(fsz(out), 96) / 2.0 + 45
            if lhsT.dtype == F32:
                c *= 3.0
            return P.op("pe", lambda e: e.matmul(out=out, lhsT=lhsT, rhs=rhs, start=start, stop=stop), reads, writes, cost=c)

        def tr(out, in_, ident, reads, writes):
            return P.op("pe", lambda e: e.transpose(out=out, in_=in_, identity=ident), reads, writes,
                        cost=260.0 if in_.dtype == F32 else 110.0)

        def rsqrt(out, in_, reads, writes):
            act(out, in_, AF.Ln, reads, writes)
            return act(out, out, AF.Exp, writes, writes, scale=-0.5)

        def v3(ap, a):
            return ap.rearrange("p (a b) -> p a b", a=a)

        ident = cst.t[:, C_ID:C_ID + 128]

        dma("sp", cst.t[:], cst_d, [], [cst.b], "cst")
        cp("dve", identb.t[:], ident, [cst.b], [identb.b])
        P.op("pool", lambda e: e.memset(onesm.t[:], 1.0 / 1024), [], [onesm.b])
        P.op("pool", lambda e: e.memset(onesb.t[:], 1.0), [], [onesb.b])
        P.op("pool", lambda e: e.memset(onesmf.t[:], 1.0 / 1024), [], [onesmf.b])
        P.op("pool", lambda e: e.memset(mh.t[:], -0.5), [], [mh.b])
        P.op("pool", lambda e: e.memset(Sst.t[:], 0.0), [], [Sst.b])
        P.op("pool", lambda e: e.memset(cb.t[:], 0.0), [], [cb.b])
        dma("sp", fin_bc.t[:], final_norm.to_broadcast([128, D]), [], [fin_bc.b], "bc0")
        dma("sp", lncg_bc.t[:], ln_c_g.to_broadcast([128, 2048]), [], [lncg_bc.b], "bc1")
        dma("sp", lncb_bc.t[:], ln_c_b.to_broadcast([128, 2048]), [], [lncb_bc.b], "bc2")
        dma("sp", lb_bc.t[:], lb_logits[0:1, :].to_broadcast([128, D]), [], [lb_bc.b], "bc3")
        dma("sp", oml_bc.t[:], lb_logits[1:2, :].to_broadcast([128, D]), [], [oml_bc.b], "bc4")
        tt("dve", lb_bc.t[:], lb_bc.t[:], oml_bc.t[:], ALU.subtract, [lb_bc.b, oml_bc.b], [lb_bc.b])
        act(lb_bc.t[:], lb_bc.t[:], AF.Sigmoid, [lb_bc.b], [lb_bc.b])
        ts("dve", oml_bc.t[:], lb_bc.t[:], -1.0, 1.0, ALU.mult, ALU.add, [lb_bc.b], [oml_bc.b])
        dma("sp", stg.t[0:31, 0:D], conv_w, [], [stg.b], "stg")
        dma("sp", stg.t[31:32, 0:D], conv_b, [], [stg.b], "stg")
        dma("sp", stg.t[32:33, 0:D], ln_a_g, [], [stg.b], "stg")
        dma("sp", stg.t[33:34, 0:D], ln_a_b, [], [stg.b], "stg")
        dma("sp", stg.t[34:35, 0:D], norm_ab, [], [stg.b], "stg")
        dma("sp", stg.t[35:36, 0:D], norm_c, [], [stg.b], "stg")
        for ch in range(8):
            tr(PW.t[:, ch * 36:(ch + 1) * 36], stg.t[0:36, ch * 128:(ch + 1) * 128], cst.t[0:36, C_ID:C_ID + 36],
               [stg.b, cst.b], [PW.b])
        cp("dve", cpar.t[:], PW.t[:, 0:8 * 36], [PW.b], [cpar.b])
        cparv = v3(cpar.t[:], 8)
        dma("sp", stg2.t[:], onorm_b, [], [stg2.b], "stg2")
        tr(PX.t[:, 0:8], stg2.t[0:8, :], cst.t[0:8, C_ID:C_ID + 8], [stg2.b, cst.b], [PX.b])
        cp("dve", gon.t[:], PX.t[:, 0:8], [PX.b], [gon.b])
        dma("sp", v3(TMP1.t[:], 8), w_s.rearrange("h t s -> t h s"), [], [TMP1.b], "wsl")
        tt("dve", v3(TMP1.t[:], 8), v3(TMP1.t[:], 8),
           cst.t[:, C_TRIL:C_TRIL + 128].unsqueeze(1).to_broadcast([128, 8, 128]), ALU.mult, [TMP1.b, cst.b], [TMP1.b])
        for h in range(8):
            tr(PW.t[:, h * 128:(h + 1) * 128], TMP1.t[:, h * 128:(h + 1) * 128], ident, [TMP1.b, cst.b], [PW.b])
        cp("dve", wsT.t[:], PW.t[:], [PW.b], [wsT.b])
        for h in range(8):
            mm(PO.t[0:8, h * 128:(h + 1) * 128], TMP1.t[0:8, h * 128:h * 128 + 8], cst.t[0:8, C_R:C_R + 128],
               True, True, [TMP1.b, cst.b], [PO.b])
        cp("dve", Xs.t[0:8, :], PO.t[0:8, :], [PO.b], [Xs.b])
        for h in range(8):
            mm(PW.t[:, h * 128:(h + 1) * 128], cst.t[0:8, C_R:C_R + 128], Xs.t[0:8, h * 128:(h + 1) * 128],
               True, True, [Xs.b, cst.b], [PW.b])
        tt("dve", v3(wsS.t[:], 8), v3(PW.t[:], 8),
           cst.t[:, C_BD:C_BD + 128].unsqueeze(1).to_broadcast([128, 8, 128]), ALU.mult, [PW.b, cst.b], [wsS.b])
        dma("sp", stg.t[0:1, 0:D], b_s.rearrange("h t -> (h t)").unsqueeze(0), [], [stg.b], "bs1")
        dma("sp", stg.t[32:33, 0:D].rearrange("p (h s t) -> p h s t", h=8, s=16),
            b_s[:, 0:8].unsqueeze(0).unsqueeze(2).to_broadcast([1, 8, 16, 8]), [], [stg.b], "bsS")
        P.op("dve", lambda e: e.memset(stg.t[0:1, D:D + 256], 1.0), [], [stg.b])
        P.op("dve", lambda e: e.memset(stg.t[32:33, D:D + 256], 1.0), [], [stg.b])
        for br_, bo_ in ((0, 0), (32, 8)):
            for hh_ in range(8):
                mm(PX.t[:, bo_ + hh_:bo_ + hh_ + 1], stg.t[br_:br_ + 1, hh_ * 128:(hh_ + 1) * 128], stg.t[br_:br_ + 1, D:D + 1],
                   True, True, [stg.b], [PX.b])
        cp("dve", bsT.t[:], PX.t[:, 0:16], [PX.b], [bsT.b])

        class Task:
            def __init__(self, gen):
                self.gen = gen
                self.done = False

        tasks = []

        def spawn(gen):
            t = Task(gen)
            tasks.append(t)
            return t

        def run_all():
            while tasks:
                for t in list(tasks):
                    try:
                        next(t.gen)
                    except StopIteration:
                        t.done = True
                        tasks.remove(t)

        def join(tl):
            while not all(t.done for t in tl):
                yield

        _A = [("glu", 2), ("glu", 3), ("val", 0), ("val", 1), ("gate", 4), ("gate", 5)]
        _B = [("f", 8), ("f", 9), ("q", 6), ("q", 7), ("i", 10), ("i", 11), ("bg", 12), ("bg", 13)]
        L0_ORDER = _A + _B if L0_AB else _B + _A
        L1_ORDER = [("v", g) for g in range(4, 8)] + [("gate", g) for g in range(8, 12)] + [("u", g) for g in range(0, 4)]
        wq = []
        n_super = SEQ // (128 * NSUB) + 1
        for _ in range(n_super):
            wq += [(w_in_ab, g * 512, 512, 8) for _, g in L0_ORDER]
            wq += [(w_out_ab, g * 256, 256, 16) for g in range(4)]
            wq += [(w_in_c, g * 512, 512, 8) for _, g in L1_ORDER]
            wq += [(w_out_c, g * 256, 256, 16) for g in range(4)]
        wst = {"prod": 0, "cons": 0}

        scr = {}
        scrB = {}
        for nm, src in (("wb_in_ab", w_in_ab), ("wb_out_ab", w_out_ab), ("wb_in_c", w_in_c), ("wb_out_c", w_out_c)):
            scr[id(src)] = nc.dram_tensor(nm, list(src.shape), BF16, kind="Internal").ap()

        def w_fill():
            while wst["prod"] < len(wq) and wst["prod"] < wst["cons"] + NSLOT:
                n = wst["prod"]
                src, c0, ncols, nk = wq[n]
                slot = wslot[n % NSLOT]
                view = slot.t[:, 0:nk * ncols].rearrange("p (k n) -> p k n", k=nk)
                key = (id(src), c0)
                if key not in scrB:
                    scrB[key] = Buf("scr%d_%d" % (len(scrB), c0))
                    dma("pool", view, src[:, c0:c0 + ncols].rearrange("(k p) n -> p k n", p=128), [], [slot.b],
                        "w%d" % (n % NSLOT))
                    dma("sp", scr[id(src)][:, c0:c0 + ncols].rearrange("(k p) n -> p k n", p=128), view, [slot.b], [scrB[key]],
                        "cv%d" % len(scrB))
                else:
                    dma(W_QUEUE, view, scr[id(src)][:, c0:c0 + ncols].rearrange("(k p) n -> p k n", p=128),
                        [scrB[key]], [slot.b], ("w%d" if W_QUEUE == "pool" else "v%d") % (n % NSLOT))
                wst["prod"] += 1

        def w_get(src, c0):
            n = wst["cons"]
            assert wq[n][0] is src and wq[n][1] == c0, (n, c0)
            w_fill()
            _, _, ncols, nk = wq[n]
            slot = wslot[n % NSLOT]
            return slot, slot.t[:, 0:nk * ncols].rearrange("p (k n) -> p k n", k=nk)

        def w_done():
            wst["cons"] += 1
            w_fill()

        pzc = {"n": 0}

        def next_pz():
            pzc["n"] += 1
            return PZ[pzc["n"] % 2]

        def rms_stats(j, src):
            P.op("dve", lambda e: e.memset(rt[j].t[:, 0:1], 0.0), [], [rt[j].b])
            act(xnb[j].t[:], src.t[:], AF.Square, [src.b, rt[j].b], [xnb[j].b, rt[j].b], accum_out=rt[j].t[:, 0:1])
            ts("dve", rt[j].t[:, 2:3], rt[j].t[:, 0:1], 1.0 / D, EPS, ALU.mult, ALU.add, [rt[j].b], [rt[j].b])
            rsqrt(rt[j].t[:, 1:2], rt[j].t[:, 2:3], [rt[j].b], [rt[j].b])

        def t_norm(j, gcol, xsrc=None, r0=0):
            if xsrc is not None:
                dma("sp", xs[j].t[:], xsrc[r0:r0 + 128, :], [], [xs[j].b], "xs%d" % j)
            rms_stats(j, xs[j])
            yield
            tsm("dve", xnb[j].t[:], xs[j].t[:], rt[j].t[:, 1:2], [xs[j].b, rt[j].b], [xnb[j].b])
            yield
            for kc in range(8):
                tr(PT.t[:, kc * 128:(kc + 1) * 128], xnb[j].t[:, kc * 128:(kc + 1) * 128], identb.t[:],
                   [xnb[j].b, identb.b], [PT.b])
            tt("dve", v3(xnT[j].t[:], 8), v3(PT.t[:], 8), cparv[:, :, gcol:gcol + 1].to_broadcast([128, 8, 128]),
               ALU.mult, [PT.b, cpar.b], [xnT[j].b])
            yield

        def proj(j, wview, nk, lhs, ncols, lhs_buf, slot):
            pz = next_pz()
            lv = v3(lhs.t[:, 0:nk * 128], nk)
            for kc in range(nk):
                mm(pz.t[:, 0:ncols], lv[:, kc, :], wview[:, kc, :], kc == 0, kc == nk - 1, [lhs_buf, slot.b], [pz.b])
            return pz

        def transpose8(src_ap, src_buf):
            for kc in range(8):
                tr(PT.t[:, kc * 128:(kc + 1) * 128], src_ap[:, kc * 128:(kc + 1) * 128], identb.t[:],
                   [src_buf, identb.b], [PT.b])

        def t_out_proj(nsub, wsrc):
            for g in range(4):
                slot, wv = w_get(wsrc, g * 256)
                for j in range(nsub):
                    pz = proj(j, wv, 16, cat[j], 256, cat[j].b, slot)
                    tt("dve", xs[j].t[:, g * 256:(g + 1) * 256], xs[j].t[:, g * 256:(g + 1) * 256], pz.t[:, 0:256],
                       ALU.add, [xs[j].b, pz.b], [xs[j].b])
                    yield
                w_done()

        def supertile(kind, nsub, tok0, is_last_prompt):
            sample = kind == "s"
            xsrc = xsm if sample else xp
            L = 8 if sample else 128 * nsub
            Lt = 128 * nsub
            nseg = 16 if sample else 1
            segw = 38 if sample else 30 + L
            cbv = cb.t[:, 0:8 * nseg * segw].rearrange("p (c s l) -> p c s l", c=8, s=nseg)
            UC, MK, UX = (C_UCS, C_MKS, C_UXS) if sample else (C_UCP, C_MKP, C_UXP)
            nch = 16 if sample else 2
            nc3 = 3 * nch
            chunks = [(8 * c, 8 * c + 8) for c in range(16)] if sample else [(0, 64), (64, 128)]

            def t_hist():
                for rt4 in range(4):
                    dma("sp", TMP2.t[0:120, :], sconv[rt4 * 120:(rt4 + 1) * 120, :], [], [TMP2.b], "sch")
                    for ch in range(8):
                        tr(PW.t[:, ch * 128:ch * 128 + 120], TMP2.t[0:120, ch * 128:(ch + 1) * 128],
                           cst.t[0:120, C_ID:C_ID + 120], [TMP2.b, cst.b], [PW.b])
                    cp("dve", cbv[:, :, rt4 * 4:(rt4 + 1) * 4, 0:30],
                       v3(PW.t[:], 8)[:, :, 0:120].rearrange("p c (s l) -> p c s l", s=4), [PW.b], [cb.b])
                    yield
                dma("sp", conv_s[:, 0:22, :], sconv.rearrange("(s r) c -> s r c", r=30)[:, 8:30, :], [], [],
                    "cs0", is_output=True)

            def t_hgrn():
                for j in range(nsub):
                    kt = KQ[j].t[:, 0:1024]; qt = KQ[j].t[:, 1024:2048]
                    vv = VS[j].t[:, 0:1024]; sbg = VS[j].t[:, 1024:2048]
                    sm = SMG[j].t[:, 0:1024]
                    qT = QKT[j].t[:, 0:1024]; kT = QKT[j].t[:, 1024:2048]
                    transpose8(qt, KQ[j].b)
                    cp("act", qT, PT.t[:], [PT.b], [QKT[j].b])
                    yield
                    transpose8(kt, KQ[j].b)
                    cp("act", kT, PT.t[:], [PT.b], [QKT[j].b])
                    yield
                    for h in range(8):
                        hs = slice(h * 128, (h + 1) * 128)
                        mm(PW.t[:, hs], kT[:, hs], qT[:, hs], True, True, [QKT[j].b], [PW.b])
                    tt("dve", v3(sm, 8), v3(PW.t[:], 8), cst.t[:, MK:MK + 128].unsqueeze(1).to_broadcast([128, 8, 128]),
                       ALU.mult, [PW.b, cst.b], [SMG[j].b])
                    yield
                    for h in range(8):
                        hs = slice(h * 128, (h + 1) * 128)
                        mm(PO.t[:, hs], vv[:, hs], sm[:, hs], h % 4 == 0, False, [VS[j].b, SMG[j].b], [PO.b])
                    yield
                    scv = v3(scal[j].t[:, 0:8 * nc3], 8)
                    for ci, (c0, c1) in enumerate(chunks):
                        if sample:
                            Ss = Sld[ci % 2]
                            dma("sp", v3(Ss.t[:], 8), shg[ci].rearrange("h d v -> d h v"), [], [Ss.b], "sld%d" % (ci % 2))
                        else:
                            Ss = Sst
                        bcol = lambda k: scv[:, :, 3 * ci + k:3 * ci + k + 1].to_broadcast([128, 8, 128])
                        tt("dve", v3(Sp.t[:], 8), v3(Ss.t[:], 8), bcol(0), ALU.mult, [Ss.b, scal[j].b], [Sp.b])
                        if sample:
                            tsm("dve", VM.t[:], vv, cst.t[:, C_CMS + ci:C_CMS + ci + 1], [VS[j].b, cst.b], [VM.b])
                        yield
                        for h in range(8):
                            hs = slice(h * 128, (h + 1) * 128)
                            if sample:
                                mm(PW.t[:, hs], kt[:, hs], VM.t[:, hs], True, True, [KQ[j].b, VM.b], [PW.b])
                            else:
                                mm(PW.t[:, hs], kt[c0:c1, hs], vv[c0:c1, hs], True, True, [KQ[j].b, VS[j].b], [PW.b])
                        for h in range(8):
                            hs = slice(h * 128, (h + 1) * 128)
                            mm(PO.t[:, h * 128 + c0:h * 128 + c1], Sp.t[:, hs], qT[:, h * 128 + c0:h * 128 + c1],
                               False, (ci == len(chunks) - 1) and (h % 4 == 3), [Sp.b, QKT[j].b], [PO.b])
                        tt("dve", v3(TMP1.t[:], 8), v3(PW.t[:], 8), bcol(1), ALU.mult, [PW.b, scal[j].b], [TMP1.b])
                        tt("dve", v3(Ss.t[:], 8), v3(Ss.t[:], 8), bcol(2), ALU.mult, [Ss.b, scal[j].b], [Ss.b])
                        tt("dve", Ss.t[:], Ss.t[:], TMP1.t[:], ALU.add, [Ss.b, TMP1.b], [Ss.b])
                        if sample:
                            dma("sp", hg_s[ci].rearrange("h d v -> d h v"), v3(Ss.t[:], 8), [Ss.b], [],
                                "sst%d" % (ci % 2), is_output=True)
                        yield
                    if is_last_prompt and j == nsub - 1:
                        dma("sp", hg_p.rearrange("h d v -> d h v"), v3(Sst.t[:], 8), [Sst.b], [], "hgp", is_output=True)
                    act(osq.t[:], PO.t[:], AF.Square, [PO.b], [osq.b])
                    yield
                    for h2 in range(2):
                        mm(PW.t[:, h2 * 512:(h2 + 1) * 512], onesb.t[:], osq.t[:, h2 * 512:(h2 + 1) * 512], True, True,
                           [onesb.b, osq.b], [PW.b])
                    ts("dve", TMP1.t[:], PW.t[:], 1.0 / 128, EPS, ALU.mult, ALU.add, [PW.b], [TMP1.b])
                    rsqrt(TMP1.t[:], TMP1.t[:], [TMP1.b], [TMP1.b])
                    yield
                    transpose8(sbg, VS[j].b)
                    tt("dve", v3(GS.t[:], 8), v3(PT.t[:], 8), gon.t[:, 0:8].unsqueeze(2).to_broadcast([128, 8, 128]), ALU.mult,
                       [PT.b, gon.b], [GS.b])
                    yield
                    tt("dve", TMP2.t[:], PO.t[:], TMP1.t[:], ALU.mult, [PO.b, TMP1.b], [TMP2.b])
                    tt(CAT_ENG, cat[j].t[:, 1024:2048], TMP2.t[:], GS.t[:], ALU.mult, [TMP2.b, GS.b], [cat[j].b])
                    yield

            def t_conv():
                for j in range(nsub):
                    for ch in range(8):
                        tr(PX.t[:, (ch % 4) * 128:(ch % 4 + 1) * 128], A[j].t[:, ch * 128:(ch + 1) * 128], ident,
                           [A[j].b, cst.b], [PX.b])
                        if ch % 4 == 3:
                            c4 = ch // 4
                            if sample:
                                cp("act", cbv[:, 4 * c4:4 * c4 + 4, :, 30:38],
                                   PX.t[:].rearrange("p (c s l) -> p c s l", c=4, s=16), [PX.b], [cb.b])
                            else:
                                cp("act", cbv[:, 4 * c4:4 * c4 + 4, 0, 30 + j * 128:30 + (j + 1) * 128], v3(PX.t[:], 4),
                                   [PX.b], [cb.b])
                            yield
                    if sample:
                        for sq in range(16):
                            dma("sp", conv_s[sq, 22:30, :], A[j].t[8 * sq:8 * sq + 8, :], [A[j].b], [], "cs1", is_output=True)
                    elif is_last_prompt and j == nsub - 1:
                        dma("sp", conv_p, A[j].t[98:128, :], [A[j].b], [], "cvp", is_output=True)
                ybv = v3(yb.t[:, 0:8 * Lt], 8)

                def win(ch, k):
                    return cbv[:, ch, :, k:k + L] if sample else cbv[:, ch, 0, k:k + L]

                def accv(ch):
                    return ybv[:, ch, :].rearrange("p (s l) -> p s l", s=16) if sample else ybv[:, ch, :]
                ysqv = v3(ysq.t[:, 0:8 * Lt], 8)
                for ch in range(8):
                    db = dgb[ch % 2]
                    dB = dgB[ch % 2]
                    dv = db.t[:].rearrange("p (k m) -> p k m", k=31)
                    for q, (k0, k1, eng) in enumerate(((0, DG_SPLIT, "dve"), (DG_SPLIT, CONV_PE_TAPS, "pool"))):
                        tt(eng, dv[:, k0:k1, :], ident.unsqueeze(1).to_broadcast([128, k1 - k0, 128]),
                           cparv[:, ch, k0:k1].unsqueeze(2).to_broadcast([128, k1 - k0, 128]), ALU.mult,
                           [cst.b, cpar.b], [dB[q]])
                    yield
                    NP = CONV_PE_TAPS
                    for k in range(NP):
                        if sample:
                            mm(PX.t[:, 0:Lt].rearrange("p (s l) -> p s l", s=16), dv[:, k, :], cbv[:, ch, :, k:k + L],
                               k == 0, k == NP - 1, [dB[0], dB[1], cb.b], [PX.b])
                        else:
                            mm(PX.t[:, 0:Lt], dv[:, k, :], cbv[:, ch, 0, k:k + L], k == 0, k == NP - 1, [dB[0], dB[1], cb.b], [PX.b])
                    bias = cparv[:, ch, 31:32]
                    act(ybv[:, ch, :], PX.t[:, 0:Lt], AF.Identity, [PX.b, cpar.b], [ybB[ch]], bias=bias)
                    if NP == 31:
                        act(ysqv[:, ch, :], PX.t[:, 0:Lt], AF.Square, [PX.b, cpar.b], [ysq.b], bias=bias)
                    yield
                    for k in range(NP, 31):
                        av = ybv[:, ch, :].rearrange("p (s l) -> p s l", s=16) if sample else ybv[:, ch, :]
                        wv_ = cbv[:, ch, :, k:k + L] if sample else cbv[:, ch, 0, k:k + L]
                        P.op("dve", (lambda av, wv_, ch, k: lambda e: e.scalar_tensor_tensor(
                            out=av, in0=wv_, scalar=cparv[:, ch, k:k + 1], in1=av, op0=ALU.mult, op1=ALU.add))(av, wv_, ch, k),
                             [cb.b, cpar.b, ybB[ch]], [ybB[ch]], cost=fsz(av) / 0.96 + 90)
                        if (k - NP) % 4 == 3:
                            yield
                    if NP < 31:
                        act(ysqv[:, ch, :], ybv[:, ch, :], AF.Square, [ybB[ch]], [ysq.b])
                        yield
                yield
                for ch in range(8):
                    mm(PW.t[:, 0:Lt], onesmf.t[:], ybv[:, ch, :], ch == 0, ch == 7, [onesmf.b, ybB[ch]], [PW.b])
                for ch in range(8):
                    mm(PW.t[:, 512:512 + Lt], onesm.t[:], v3(ysq.t[:, 0:8 * Lt], 8)[:, ch, :], ch == 0, ch == 7,
                       [onesm.b, ysq.b], [PW.b])
                cp("act", MEAN.t[:, 0:Lt], PW.t[:, 0:Lt], [PW.b], [MEAN.b])
                tt("dve", VAR.t[:, 0:Lt], MEAN.t[:, 0:Lt], MEAN.t[:, 0:Lt], ALU.mult, [MEAN.b], [VAR.b])
                tt("dve", VAR.t[:, 0:Lt], PW.t[:, 512:512 + Lt], VAR.t[:, 0:Lt], ALU.subtract, [PW.b, VAR.b], [VAR.b])
                ts("dve", VAR.t[:, 0:Lt], VAR.t[:, 0:Lt], 1.0, EPS, ALU.mult, ALU.add, [VAR.b], [VAR.b])
                rsqrt(RSTD.t[:, 0:Lt], VAR.t[:, 0:Lt], [VAR.b], [RSTD.b])
                yield
                tt("dve", ybv, ybv, MEAN.t[:, 0:Lt].unsqueeze(1).to_broadcast([128, 8, Lt]), ALU.subtract,
                   ybB + [MEAN.b], ybB)
                tt("dve", ybv, ybv, RSTD.t[:, 0:Lt].unsqueeze(1).to_broadcast([128, 8, Lt]), ALU.mult, ybB + [RSTD.b], ybB)
                yield
                for ch in range(8):
                    act(ybv[:, ch, :], ybv[:, ch, :], AF.Silu, [ybB[ch], cpar.b], [ybB[ch]],
                        scale=cparv[:, ch, 32:33], bias=cparv[:, ch, 33:34])
                yield
                for j in range(nsub):
                    transpose8(SMG[j].t[:, 1024:2048], SMG[j].b)
                    tt("dve", v3(cat[j].t[:, 0:1024], 8), ybv[:, :, j * 128:(j + 1) * 128], v3(PT.t[:], 8), ALU.mult,
                       ybB + [PT.b], [cat[j].b])
                    yield
                if not sample:
                    cp("act", cbv[:, :, 0, 0:30], cbv[:, :, 0, L:L + 30], [cb.b], [cb.b])

            def t_vln(j):
                VG = FE[j]
                r = rt[j]
                act(KQ[j].t[:], VG.t[:], AF.Square, [VG.b, r.b], [KQ[j].b, r.b], accum_out=r.t[:, 8:9])
                P.op("dve", lambda e: e.tensor_reduce(out=r.t[:, 9:10], in_=r.t[:, 4:8], axis=AX.X, op=ALU.add), [r.b], [r.b])
                ts("dve", r.t[:, 9:10], r.t[:, 9:10], 1.0 / 2048, 0.0, ALU.mult, ALU.add, [r.b], [r.b])
                tt("dve", r.t[:, 10:11], r.t[:, 9:10], r.t[:, 9:10], ALU.mult, [r.b], [r.b])
                ts("dve", r.t[:, 11:12], r.t[:, 8:9], 1.0 / 2048, EPS, ALU.mult, ALU.add, [r.b], [r.b])
                tt("dve", r.t[:, 11:12], r.t[:, 11:12], r.t[:, 10:11], ALU.subtract, [r.b], [r.b])
                rsqrt(r.t[:, 12:13], r.t[:, 11:12], [r.b], [r.b])
                tt("dve", r.t[:, 13:14], r.t[:, 9:10], r.t[:, 12:13], ALU.mult, [r.b], [r.b])
                ts("dve", r.t[:, 13:14], r.t[:, 13:14], -1.0, 0.0, ALU.mult, ALU.add, [r.b], [r.b])
                yield
                act(VG.t[:], VG.t[:], AF.Identity, [VG.b, r.b], [VG.b], scale=r.t[:, 12:13], bias=r.t[:, 13:14])
                yield
                if VLN_SPLIT:
                    for eng_, sl_ in (("pool", slice(0, VLN_SPLIT)), ("dve", slice(VLN_SPLIT, 2048))):
                        tt(eng_, VG.t[:, sl_], VG.t[:, sl_], lncg_bc.t[:, sl_], ALU.mult, [VG.b, lncg_bc.b], [VG.b])
                    yield
                    for eng_, sl_ in (("pool", slice(0, VLN_SPLIT)), ("dve", slice(VLN_SPLIT, 2048))):
                        tt(eng_, VG.t[:, sl_], VG.t[:, sl_], lncb_bc.t[:, sl_], ALU.add, [VG.b, lncb_bc.b], [VG.b])
                else:
                    tt("pool", VG.t[:], VG.t[:], lncg_bc.t[:], ALU.mult, [VG.b, lncg_bc.b], [VG.b])
                    yield
                    tt("dve", VG.t[:], VG.t[:], lncb_bc.t[:], ALU.add, [VG.b, lncb_bc.b], [VG.b])
                yield
                cp("act", VS[j].t[:], VG.t[:], [VG.b], [VS[j].b])
                if sample:
                    dma("sp", gv_s, VG.t[:], [VG.b], [], "gvs", is_output=True)
                elif is_last_prompt and j == nsub - 1:
                    dma("sp", gv_p, VG.t[:], [VG.b], [], "gvp", is_output=True)

            def t_mix():
                wsX = wsS if sample else wsT
                br = 32 if sample else 0
                for j in range(nsub):
                    for half in range(2):
                        PH = PW if half == 0 else PO
                        for hl in range(4):
                            h = half * 4 + hl
                            mm(PH.t[:, hl * 256:(hl + 1) * 256], wsX.t[:, h * 128:(h + 1) * 128], VS[j].t[:, h * 256:(h + 1) * 256],
                               True, True, [wsX.b, VS[j].b], [PH.b])
                        bo = 8 if sample else 0
                        for hl in range(4):
                            h = half * 4 + hl
                            P.op("dve", (lambda j, h, hl, PH, bo: lambda e: e.scalar_tensor_tensor(
                                out=KQ[j].t[:, h * 256:(h + 1) * 256], in0=PH.t[:, hl * 256:(hl + 1) * 256],
                                scalar=bsT.t[:, bo + h:bo + h + 1], in1=SMG[j].t[:, h * 256:(h + 1) * 256],
                                op0=ALU.add, op1=ALU.mult))(j, h, hl, PH, bo),
                                 [PH.b, SMG[j].b, bsT.b], [KQ[j].b], cost=360.0)
                        yield
                    for half in range(2):
                        transpose8(KQ[j].t[:, half * 1024:(half + 1) * 1024], KQ[j].b)
                        cp("act", cat[j].t[:, half * 1024:(half + 1) * 1024], PT.t[:], [PT.b], [cat[j].b])
                        yield

            tl = [spawn(t_norm(j, 34, xsrc, tok0 + j * 128)) for j in range(nsub)]
            if sample:
                tl.append(spawn(t_hist()))
            yield from join(tl)
            mix_tasks = []
            for kindg, g in L0_ORDER:
                hh = g % 2
                cs = slice(hh * 512, (hh + 1) * 512)
                slot, wv = w_get(w_in_ab, g * 512)
                for j in range(nsub):
                    pz = proj(j, wv, 8, xnT[j], 512, xnT[j].b, slot)
                    F = FE[j].t[:, 0:1024]; E1 = FE[j].t[:, 1024:2048]
                    if kindg == "f":
                        act(F[:, cs], pz.t[:], AF.Sigmoid, [pz.b], [FE[j].b])
                        if hh == 1:
                            yield
                            tt("dve", F, F, oml_bc.t[:], ALU.mult, [FE[j].b, oml_bc.b], [FE[j].b])
                            tt("dve", F, F, lb_bc.t[:], ALU.add, [FE[j].b, lb_bc.b], [FE[j].b])
                            ts("dve", kk[j].t[:], F, -1.0, 1.0, ALU.mult, ALU.add, [FE[j].b], [kk[j].b])
                            act(F, F, AF.Ln, [FE[j].b], [FE[j].b])
                            yield
                            for h2 in range(2):
                                mm(PW.t[:, h2 * 512:(h2 + 1) * 512], cst.t[:, UC:UC + 128], F[:, h2 * 512:(h2 + 1) * 512],
                                   True, True, [cst.b, FE[j].b], [PW.b])
                            for h in range(8):
                                mm(PX.t[:, h * nc3:(h + 1) * nc3], F[:, h * 128:(h + 1) * 128], cst.t[:, UX:UX + nc3],
                                   True, True, [cst.b, FE[j].b], [PX.b])
                            act(E1, PW.t[:], AF.Exp, [PW.b], [FE[j].b])
                            act(TMP1.t[:], PW.t[:], AF.Exp, [PW.b], [TMP1.b], scale=-1.0)
                            act(scal[j].t[:, 0:8 * nc3], PX.t[:, 0:8 * nc3], AF.Exp, [PX.b], [scal[j].b])
                            tt("dve", KQ[j].t[:, 0:1024], kk[j].t[:], TMP1.t[:], ALU.mult, [kk[j].b, TMP1.b], [KQ[j].b])
                    elif kindg == "q":
                        tt("dve", KQ[j].t[:, 1024 + hh * 512:1024 + (hh + 1) * 512], pz.t[:], E1[:, cs], ALU.mult,
                           [pz.b, FE[j].b], [KQ[j].b])
                    elif kindg == "i":
                        act(VS[j].t[:, cs], pz.t[:], AF.Silu, [pz.b], [VS[j].b])
                    elif kindg == "bg":
                        act(VS[j].t[:, 1024 + hh * 512:1024 + (hh + 1) * 512], pz.t[:], AF.Silu, [pz.b], [VS[j].b])
                    elif kindg == "glu":
                        act(A[j].t[:, cs], pz.t[:], AF.Sigmoid, [pz.b], [A[j].b])
                    elif kindg == "val":
                        tt("dve", A[j].t[:, cs], pz.t[:], A[j].t[:, cs], ALU.mult, [pz.b, A[j].b], [A[j].b])
                    elif kindg == "gate":
                        act(SMG[j].t[:, 1024 + hh * 512:1024 + (hh + 1) * 512], pz.t[:], AF.Silu, [pz.b], [SMG[j].b])
                    yield
                w_done()
                if (kindg, g) == ("gate", 5):
                    mix_tasks.append(spawn(t_conv()))
                if (kindg, g) == ("bg", 13):
                    mix_tasks.append(spawn(t_hgrn()))
            yield from join(mix_tasks)
            yield from t_out_proj(nsub, w_out_ab)
            yield from join([spawn(t_norm(j, 35)) for j in range(nsub)])
            vtasks = []
            for kindg, g in L1_ORDER:
                gi = g % 4
                cs = slice(gi * 512, (gi + 1) * 512)
                slot, wv = w_get(w_in_c, g * 512)
                for j in range(nsub):
                    pz = proj(j, wv, 8, xnT[j], 512, xnT[j].b, slot)
                    VG = FE[j]; SG = QKT[j]; UG = SMG[j]
                    if kindg == "v":
                        if gi == 0:
                            P.op("dve", lambda e, j=j: e.memset(rt[j].t[:, 4:12], 0.0), [], [rt[j].b])
                        act(VG.t[:, cs], pz.t[:], AF.Gelu_apprx_tanh, [pz.b, rt[j].b], [VG.b, rt[j].b],
                            accum_out=rt[j].t[:, 4 + gi:5 + gi])
                        if gi == 3:
                            vtasks.append(spawn(t_vln(j)))
                    elif kindg == "gate":
                        act(SG.t[:, cs], pz.t[:], AF.Silu, [pz.b], [SG.b])
                    else:
                        act(UG.t[:, cs], pz.t[:], AF.Gelu_apprx_tanh, [pz.b], [UG.b])
                        if gi == 3:
                            tt("dve", UG.t[:], UG.t[:], SG.t[:], ALU.mult, [UG.b, SG.b], [UG.b])
                    yield
                w_done()
            yield from join(vtasks)
            yield from t_mix()
            yield from t_out_proj(nsub, w_out_c)
            for j in range(nsub):
                rms_stats(j, xs[j])
                P.op("dve", lambda e, j=j: e.scalar_tensor_tensor(out=TMP2.t[:], in0=xs[j].t[:], scalar=rt[j].t[:, 1:2],
                                                                  in1=fin_bc.t[:], op0=ALU.mult, op1=ALU.mult),
                     [xs[j].b, rt[j].b, fin_bc.b], [TMP2.b])
                r0 = tok0 + j * 128
                dst = y_s if sample else y_p
                dma("sp", dst[r0:r0 + 128, :], TMP2.t[:], [TMP2.b], [], "yout", is_output=True)
                yield

        def main():
            nst = SEQ // (128 * NSUB)
            for s_i in range(nst):
                yield from supertile("p", NSUB, s_i * 128 * NSUB, s_i == nst - 1)
            yield from supertile("s", 1, 0, False)

        w_fill()
        spawn(main())
        run_all()
        assert wst["cons"] == len(wq), (wst, len(wq))
        if LIST_SCHED:
            P.schedule(SCHED_WINDOW)
        P.emit(nc, st)
    return nc


_CACHE = {}


def kernel(**inputs):
    f = lambda a: np.ascontiguousarray(np.asarray(a, dtype=np.float32))
    x_prompt = f(inputs["x_prompt"]); x_sample = f(inputs["x_sample"])
    state_conv = f(inputs["state_conv"]); state_hgrn = f(inputs["state_hgrn"])
    if "nc" not in _CACHE:
        _CACHE["nc"] = build_program()
    nc = _CACHE["nc"]
    shared = {
        "norm_ab": f(inputs["norm_ab"]).reshape(1, D), "w_in_ab": f(inputs["w_in_ab"])[0],
        "conv_w": f(inputs["conv_w"])[0], "conv_b": f(inputs["conv_b"]).reshape(1, D),
        "ln_a_g": f(inputs["ln_a_g"]).reshape(1, D), "ln_a_b": f(inputs["ln_a_b"]).reshape(1, D),
        "lb_logits": f(inputs["lb_logits"]), "onorm_b": f(inputs["onorm_b"])[0],
        "w_out_ab": f(inputs["w_out_ab"])[0], "norm_c": f(inputs["norm_c"]).reshape(1, D),
        "w_in_c": f(inputs["w_in_c"])[0], "ln_c_g": f(inputs["ln_c_g"]).reshape(1, 2048),
        "ln_c_b": f(inputs["ln_c_b"]).reshape(1, 2048), "w_s": f(inputs["w_s"])[0], "b_s": f(inputs["b_s"])[0],
        "w_out_c": f(inputs["w_out_c"])[0], "final_norm": f(inputs["final_norm"]).reshape(1, D),
        "cst": make_consts(),
    }
    in_maps = []
    for c in range(NCORES):
        m = dict(shared)
        m["xp"] = x_prompt[c]
        m["xsm"] = x_sample[16 * c:16 * (c + 1)].reshape(128, D)
        m["sconv"] = state_conv[0, 16 * c:16 * (c + 1)].reshape(480, D)
        m["shg"] = state_hgrn[0, 16 * c:16 * (c + 1)]
        in_maps.append(m)
    res = run_bass_kernel_spmd(nc, in_maps, core_ids=list(range(NCORES)))
    R = res.results
    y_prompt = np.stack([R[c]["y_p"] for c in range(NCORES)], 0)
    y_sample = np.concatenate([R[c]["y_s"].reshape(16, 8, D) for c in range(NCORES)], 0)
    conv_prompt = np.stack([R[c]["conv_p"] for c in range(NCORES)], 0)[None]
    hgrn_prompt = np.stack([R[c]["hg_p"] for c in range(NCORES)], 0)[None]
    gv_prompt = np.stack([R[c]["gv_p"] for c in range(NCORES)], 0)[None]
    conv_sample = np.concatenate([R[c]["conv_s"] for c in range(NCORES)], 0)[None]
    hgrn_sample = np.concatenate([R[c]["hg_s"] for c in range(NCORES)], 0)[None]
    gv_sample = np.concatenate([R[c]["gv_s"].reshape(16, 8, 2048) for c in range(NCORES)], 0)[None]
    return tuple(np.ascontiguousarray(a, dtype=np.float32) for a in
                 (y_prompt, y_sample, conv_prompt, hgrn_prompt, gv_prompt, conv_sample, hgrn_sample, gv_sample))
```

```python
import contextlib
import numpy as np
import concourse.bass as bass
import concourse.mybir as mybir
from concourse.bass_utils import run_bass_kernel_spmd

F32 = mybir.dt.float32
BF16 = mybir.dt.bfloat16
AF = mybir.ActivationFunctionType
ALU = mybir.AluOpType
AX = mybir.AxisListType

NCORES = 8
D = 1024
SEQ = 2048
NSUB = 2
NARENA = 2
LIST_SCHED = True
ACT_SWITCH_NS = 1300.0
ACT_SWITCH_COST = 2600.0
DG_SPLIT = 20
VLN_SPLIT = 640
CAT_ENG = "dve"
W_QUEUE = "sp"
L0_AB = True
CONV_PE_TAPS = 31
SCHED_WINDOW = 100
SAMPLE_ALONE = True
NSLOT = 3
EPS = 1e-6

C_ID, C_TRIL, C_UCP, C_MKP, C_UXP = 0, 128, 256, 384, 512
C_UCS, C_MKS, C_UXS, C_CMS, C_BD, C_R = 520, 648, 776, 824, 840, 968
NCST = 1096


def make_consts():
    c = np.zeros((128, NCST), np.float32)
    i = np.arange(128)
    c[:, C_ID:C_ID + 128] = np.eye(128)
    c[:, C_TRIL:C_TRIL + 128] = (i[None, :] <= i[:, None])

    def fill(clen, ucol, mcol, xcol):
        ch = i // clen
        mid = ch * clen + clen // 2
        same = ch[:, None] == ch[None, :]
        c[:, ucol:ucol + 128] = same * ((i[:, None] <= i[None, :]).astype(np.float32)
                                        - (i[:, None] <= mid[None, :]).astype(np.float32))
        c[:, mcol:mcol + 128] = same * (i[:, None] <= i[None, :])
        for cc in range(128 // clen):
            inc = ch == cc
            c[:, xcol + 3 * cc + 0] = inc * (i <= mid)
            c[:, xcol + 3 * cc + 1] = inc * (i > mid)
            c[:, xcol + 3 * cc + 2] = inc
    fill(64, C_UCP, C_MKP, C_UXP)
    fill(8, C_UCS, C_MKS, C_UXS)
    for cc in range(16):
        c[:, C_CMS + cc] = (i // 8 == cc)
    c[:, C_BD:C_BD + 128] = (i[:, None] // 8 == i[None, :] // 8)
    for a in range(8):
        c[a, C_R:C_R + 128] = (i % 8 == a)
    return c


class Buf:
    __slots__ = ("name", "writer", "readers")

    def __init__(self, name):
        self.name = name
        self.writer = None
        self.readers = []


class Op:
    __slots__ = ("eng", "fn", "deps", "ticket", "has_dep", "dma_key", "idx", "cost", "nbytes", "fin", "seq", "wn", "tbl")

    def __init__(self, eng, fn, deps, dma_key=None):
        self.cost = 100.0
        self.nbytes = 0
        self.tbl = 0
        self.fin = None
        self.seq = 0
        self.eng = eng
        self.fn = fn
        self.deps = deps
        self.ticket = None
        self.has_dep = False
        self.dma_key = dma_key
        self.idx = None


class Prog:
    ENGS = ("pe", "act", "dve", "pool", "sp")

    def __init__(self):
        self.ops = {e: [] for e in self.ENGS}
        self.dma_keys = {}
        self.outs = []

    def op(self, eng, fn, reads=(), writes=(), dma_key=None, is_output=False, cost=100.0, nbytes=0):
        d = set()
        for b in reads:
            if b.writer is not None:
                d.add(b.writer)
        for b in writes:
            if b.writer is not None:
                d.add(b.writer)
            d.update(b.readers)
        o = Op(eng, fn, d, dma_key)
        o.cost = cost
        o.nbytes = nbytes
        o.wn = ",".join(b.name for b in writes) + "<" + ",".join(b.name for b in reads)
        self.nops = getattr(self, "nops", 0) + 1
        o.seq = self.nops
        o.idx = len(self.ops[eng])
        self.ops[eng].append(o)
        for b in reads:
            b.readers.append(o)
        for b in writes:
            b.writer = o
            b.readers = []
        if dma_key is not None:
            self.dma_keys.setdefault(dma_key, eng)
            assert self.dma_keys[dma_key] == eng
        if is_output:
            self.outs.append(o)
        return o

    def schedule(self, window=40):
        lists = {e: list(self.ops[e]) for e in self.ENGS}
        out = {e: [] for e in self.ENGS}
        free = {e: 0.0 for e in self.ENGS}
        hbm_free = 0.0
        cur_tbl = [0]
        remaining = sum(len(v) for v in lists.values())
        while remaining:
            best = None
            for e in self.ENGS:
                L = lists[e]
                for i in range(min(window, len(L))):
                    o = L[i]
                    rdy = 0.0
                    ok = True
                    for dd in o.deps:
                        if dd.fin is None:
                            ok = False
                            break
                        lat = dd.fin if dd.eng == e and dd.dma_key is None else dd.fin + 70.0
                        if lat > rdy:
                            rdy = lat
                    if not ok:
                        continue
                    start = max(free[e], rdy)
                    sw = e == "act" and o.tbl and o.tbl != cur_tbl[0]
                    pen = ACT_SWITCH_NS if sw else 0.0
                    key = (start + pen, o.seq)
                    if best is None or key < best[0]:
                        best = (key, e, i, o, start + (ACT_SWITCH_COST if sw else 0.0))
                    if start <= free[e] and not pen:
                        break
            _, e, i, o, start = best
            lists[e].pop(i)
            out[e].append(o)
            if e == "act" and o.tbl:
                cur_tbl[0] = o.tbl
            if o.dma_key is not None:
                free[e] = start + o.cost
                t0 = max(free[e], hbm_free)
                hbm_free = t0 + o.nbytes / 330.0
                o.fin = hbm_free + 1800.0
            else:
                free[e] = start + o.cost
                o.fin = free[e]
            remaining -= 1
        for e in self.ENGS:
            for k, o in enumerate(out[e]):
                o.idx = k
            self.ops[e] = out[e]
        self.est_ns = max(o.fin for e in self.ENGS for o in self.ops[e])

    def emit(self, nc, stack):
        def skip(dd, o):
            return dd.dma_key is None and o.dma_key is None and dd.eng == "pe" and o.eng == "pe"
        def needed(o):
            last = {}
            out = []
            for dd in o.deps:
                if skip(dd, o):
                    continue
                if dd.dma_key is not None:
                    out.append(dd)
                elif dd.eng not in last or dd.idx > last[dd.eng].idx:
                    last[dd.eng] = dd
            return out + list(last.values())
        for e in self.ENGS:
            for k, o in enumerate(self.ops[e]):
                o.idx = k
        for e in self.ENGS:
            for o in self.ops[e]:
                for dd in needed(o):
                    dd.has_dep = True
        sems = {e: stack.enter_context(nc.semaphore("s_" + e)) for e in ("pe", "act", "dve", "pool")}
        dsems = {k: stack.enter_context(nc.semaphore("d_" + k)) for k in self.dma_keys}
        dcnt = {k: 0 for k in self.dma_keys}
        for e in self.ENGS:
            cnt = 0
            for o in self.ops[e]:
                if o.dma_key is not None:
                    dcnt[o.dma_key] += 16
                    o.ticket = dcnt[o.dma_key]
                elif o.has_dep:
                    cnt += 1
                    o.ticket = cnt
        fw = {}
        for o in self.outs:
            fw[o.dma_key] = max(fw.get(o.dma_key, 0), o.ticket)
        block = stack.enter_context(nc.Block())
        handles = {"pe": "tensor", "act": "scalar", "dve": "vector", "pool": "gpsimd", "sp": "sync"}

        def make_body(e):
            def body(eng):
                waited = {}
                for o in self.ops[e]:
                    for dd in sorted(needed(o), key=lambda z: (z.eng, z.idx)):
                        if dd.dma_key is not None:
                            key, sem = ("d", dd.dma_key), dsems[dd.dma_key]
                        else:
                            key, sem = ("e", dd.eng), sems[dd.eng]
                        if waited.get(key, 0) >= dd.ticket:
                            continue
                        eng.wait_ge(sem, dd.ticket)
                        waited[key] = dd.ticket
                    ins = o.fn(eng)
                    if o.dma_key is not None:
                        ins.then_inc(dsems[o.dma_key], 16)
                    elif o.has_dep:
                        ins.then_inc(sems[e], 1)
                if e == "sp":
                    for k, t in fw.items():
                        eng.wait_ge(dsems[k], t)
            return body

        for e in self.ENGS:
            getattr(block, handles[e])(make_body(e))


class T:
    def __init__(self, t, name, buf=None):
        self.t = t
        self.b = buf if buf is not None else Buf(name)


def build_program():
    nc = bass.Bass("TRN2", target_bir_lowering=False)
    din = lambda n, s: nc.dram_tensor(n, list(s), F32, kind="ExternalInput").ap()
    dout = lambda n, s: nc.dram_tensor(n, list(s), F32, kind="ExternalOutput").ap()
    xp = din("xp", (SEQ, D)); xsm = din("xsm", (128, D))
    sconv = din("sconv", (16 * 30, D)); shg = din("shg", (16, 8, 128, 128))
    norm_ab = din("norm_ab", (1, D)); w_in_ab = din("w_in_ab", (D, 7168))
    conv_w = din("conv_w", (31, D)); conv_b = din("conv_b", (1, D))
    ln_a_g = din("ln_a_g", (1, D)); ln_a_b = din("ln_a_b", (1, D))
    lb_logits = din("lb_logits", (2, D)); onorm_b = din("onorm_b", (8, 128))
    w_out_ab = din("w_out_ab", (2048, D)); norm_c = din("norm_c", (1, D))
    w_in_c = din("w_in_c", (D, 6144)); ln_c_g = din("ln_c_g", (1, 2048)); ln_c_b = din("ln_c_b", (1, 2048))
    w_s = din("w_s", (8, 128, 128)); b_s = din("b_s", (8, 128))
    w_out_c = din("w_out_c", (2048, D)); final_norm = din("final_norm", (1, D))
    cst_d = din("cst", (128, NCST))
    y_p = dout("y_p", (SEQ, D)); y_s = dout("y_s", (128, D))
    conv_p = dout("conv_p", (30, D)); hg_p = dout("hg_p", (8, 128, 128)); gv_p = dout("gv_p", (128, 2048))
    conv_s = dout("conv_s", (16, 30, D)); hg_s = dout("hg_s", (16, 8, 128, 128)); gv_s = dout("gv_s", (128, 2048))

    P = Prog()
    with contextlib.ExitStack() as st:
        def sb(name, cols, dt=F32, parts=128):
            return T(st.enter_context(nc.sbuf_tensor("sb_" + name, [parts, cols], dt)), name)

        def ps(name, cols, dt=F32):
            return T(st.enter_context(nc.psum_tensor("ps_" + name, [128, cols], dt)), name)

        PZ = [ps("PZ0", 512), ps("PZ1", 512)]
        PT = ps("PT", 1024, BF16)
        PW = ps("PW", 1024)
        PO = ps("PO", 1024)
        PX = ps("PX", 512)
        cst = sb("cst", NCST)
        identb = sb("identb", 128, BF16)
        onesm = sb("onesm", 128, BF16)
        onesb = sb("onesb", 128, BF16)
        onesmf = sb("onesmf", 128)
        mh = sb("mh", 1)
        lb_bc = sb("lb_bc", D); oml_bc = sb("oml_bc", D); fin_bc = sb("fin_bc", D)
        lncg_bc = sb("lncg_bc", 2048); lncb_bc = sb("lncb_bc", 2048)
        stg = sb("stg", D + 256, F32, parts=36)
        cpar = sb("cpar", 8 * 36)
        stg2 = sb("stg2", 128, F32, parts=8)
        gon = sb("gon", 8)
        bsT = sb("bsT", 16)
        wsT = sb("wsT", 1024, BF16); wsS = sb("wsS", 1024, BF16)
        wslot = [sb("wslot%d" % i, 8 * 512, BF16) for i in range(NSLOT)]
        xs = [sb("xs%d" % j, D) for j in range(NARENA)]
        xnb = [sb("xnb%d" % j, D, BF16) for j in range(NARENA)]
        xnT = [sb("xnT%d" % j, D, BF16) for j in range(NARENA)]
        rt = [sb("rt%d" % j, 16) for j in range(NARENA)]
        FE = [sb("FE%d" % j, 2048) for j in range(NARENA)]
        kk = xnb
        KQ = [sb("KQ%d" % j, 2048, BF16) for j in range(NARENA)]
        VS = [sb("VS%d" % j, 2048, BF16) for j in range(NARENA)]
        A = [sb("A%d" % j, D) for j in range(NARENA)]
        SMG = [sb("SMG%d" % j, 2048, BF16) for j in range(NARENA)]
        QKT = [sb("QKT%d" % j, 2048, BF16) for j in range(NARENA)]
        cat = [sb("cat%d" % j, 2048, BF16) for j in range(NARENA)]
        scal = [sb("scal%d" % j, 8 * 48) for j in range(NARENA)]
        TMP1 = sb("TMP1", D); TMP2 = sb("TMP2", D)
        Sst = sb("Sst", D); Sp = sb("Sp", D, BF16); osq = sb("osq", D, BF16)
        GS = Sp; VM = osq; Xs = TMP2; bs1 = stg
        SA = (SEQ // 128) % NARENA
        Sld = [A[1], xs[1]]
        LMAX = 128 * NSUB
        yb = sb("yb", 8 * LMAX); ysq = sb("ysq", 8 * LMAX, BF16)
        ybB = [Buf("yb%d" % c) for c in range(8)]
        yb2B = [Buf("yb2_%d" % c) for c in range(8)]
        dgb = [sb("dgb%d" % i, 31 * 128, BF16) for i in range(2)]
        dgB = [[Buf("dgb%d_%d" % (i, q)) for q in range(2)] for i in range(2)]
        MEAN = sb("MEAN", LMAX); VAR = sb("VAR", LMAX); RSTD = sb("RSTD", LMAX)
        CBW = max(30 + LMAX, 16 * 38)
        cb = sb("cb", 8 * CBW, BF16)
        def fsz(ap):
            n = 1
            for s_ in ap.shape[1:]:
                n *= int(s_)
            return n

        def esz(ap):
            return 2 if ap.dtype == BF16 else 4

        def dma(eng, out, in_, reads, writes, key, is_output=False):
            nb = int(out.shape[0]) * fsz(out) * max(esz(out), esz(in_))
            return P.op(eng, lambda e: e.dma_start(out=out, in_=in_), reads, writes, dma_key=key, is_output=is_output,
                        cost=1100.0 if eng == "pool" else 80.0, nbytes=nb)

        ACT_TBL = {AF.Sigmoid: 1, AF.Silu: 2, AF.Ln: 3, AF.Exp: 3, AF.Gelu_apprx_tanh: 4}

        def act(out, in_, func, reads, writes, **kw):
            o = P.op("act", lambda e: e.activation(out=out, in_=in_, func=func, **kw), reads, writes,
                     cost=(fsz(out) + 260) / 1.2)
            o.tbl = ACT_TBL.get(func, 0)
            return o

        def vcost(eng, out):
            n = fsz(out)
            return n * 2.3 + 120 if eng == "pool" else n / 0.96 + 90

        def tt(eng, out, in0, in1, op, reads, writes):
            return P.op(eng, lambda e: e.tensor_tensor(out=out, in0=in0, in1=in1, op=op), reads, writes, cost=vcost(eng, out))

        def ts(eng, out, in0, s1, s2, op0, op1, reads, writes):
            return P.op(eng, lambda e: e.tensor_scalar(out=out, in0=in0, scalar1=s1, scalar2=s2, op0=op0, op1=op1),
                        reads, writes, cost=vcost(eng, out))

        def tsm(eng, out, in0, s1, reads, writes):
            return P.op(eng, lambda e: e.tensor_scalar_mul(out=out, in0=in0, scalar1=s1), reads, writes, cost=vcost(eng, out))

        def cp(eng, out, in_, reads, writes):
            if eng == "act":
                return P.op(eng, lambda e: e.activation(out=out, in_=in_, func=AF.Copy), reads, writes,
                            cost=(fsz(out) + 260) / 1.2)
            return P.op(eng, lambda e: e.tensor_copy(out=out, in_=in_), reads, writes, cost=vcost(eng, out))

        def mm(out, lhsT, rhs, start, stop, reads, writes):
            c = max(fsz(out), 96) / 2.0 + 45
            if lhsT.dtype == F32:
                c *= 3.0
            return P.op("pe", lambda e: e.matmul(out=out, lhsT=lhsT, rhs=rhs, start=start, stop=stop), reads, writes, cost=c)

        def tr(out, in_, ident, reads, writes):
            return P.op("pe", lambda e: e.transpose(out=out, in_=in_, identity=ident), reads, writes,
                        cost=260.0 if in_.dtype == F32 else 110.0)

        def rsqrt(out, in_, reads, writes):
            act(out, in_, AF.Ln, reads, writes)
            return act(out, out, AF.Exp, writes, writes, scale=-0.5)

        def v3(ap, a):
            return ap.rearrange("p (a b) -> p a b", a=a)

        ident = cst.t[:, C_ID:C_ID + 128]

        dma("sp", cst.t[:], cst_d, [], [cst.b], "cst")
        cp("dve", identb.t[:], ident, [cst.b], [identb.b])
        P.op("pool", lambda e: e.memset(onesm.t[:], 1.0 / 1024), [], [onesm.b])
        P.op("pool", lambda e: e.memset(onesb.t[:], 1.0), [], [onesb.b])
        P.op("pool", lambda e: e.memset(onesmf.t[:], 1.0 / 1024), [], [onesmf.b])
        P.op("pool", lambda e: e.memset(mh.t[:], -0.5), [], [mh.b])
        P.op("pool", lambda e: e.memset(Sst.t[:], 0.0), [], [Sst.b])
        P.op("pool", lambda e: e.memset(cb.t[:], 0.0), [], [cb.b])
        dma("sp", fin_bc.t[:], final_norm.to_broadcast([128, D]), [], [fin_bc.b], "bc0")
        dma("sp", lncg_bc.t[:], ln_c_g.to_broadcast([128, 2048]), [], [lncg_bc.b], "bc1")
        dma("sp", lncb_bc.t[:], ln_c_b.to_broadcast([128, 2048]), [], [lncb_bc.b], "bc2")
        dma("sp", lb_bc.t[:], lb_logits[0:1, :].to_broadcast([128, D]), [], [lb_bc.b], "bc3")
        dma("sp", oml_bc.t[:], lb_logits[1:2, :].to_broadcast([128, D]), [], [oml_bc.b], "bc4")
        tt("dve", lb_bc.t[:], lb_bc.t[:], oml_bc.t[:], ALU.subtract, [lb_bc.b, oml_bc.b], [lb_bc.b])
        act(lb_bc.t[:], lb_bc.t[:], AF.Sigmoid, [lb_bc.b], [lb_bc.b])
        ts("dve", oml_bc.t[:], lb_bc.t[:], -1.0, 1.0, ALU.mult, ALU.add, [lb_bc.b], [oml_bc.b])
        dma("sp", stg.t[0:31, 0:D], conv_w, [], [stg.b], "stg")
        dma("sp", stg.t[31:32, 0:D], conv_b, [], [stg.b], "stg")
        dma("sp", stg.t[32:33, 0:D], ln_a_g, [], [stg.b], "stg")
        dma("sp", stg.t[33:34, 0:D], ln_a_b, [], [stg.b], "stg")
        dma("sp", stg.t[34:35, 0:D], norm_ab, [], [stg.b], "stg")
        dma("sp", stg.t[35:36, 0:D], norm_c, [], [stg.b], "stg")
        for ch in range(8):
            tr(PW.t[:, ch * 36:(ch + 1) * 36], stg.t[0:36, ch * 128:(ch + 1) * 128], cst.t[0:36, C_ID:C_ID + 36],
               [stg.b, cst.b], [PW.b])
        cp("dve", cpar.t[:], PW.t[:, 0:8 * 36], [PW.b], [cpar.b])
        cparv = v3(cpar.t[:], 8)
        dma("sp", stg2.t[:], onorm_b, [], [stg2.b], "stg2")
        tr(PX.t[:, 0:8], stg2.t[0:8, :], cst.t[0:8, C_ID:C_ID + 8], [stg2.b, cst.b], [PX.b])
        cp("dve", gon.t[:], PX.t[:, 0:8], [PX.b], [gon.b])
        dma("sp", v3(TMP1.t[:], 8), w_s.rearrange("h t s -> t h s"), [], [TMP1.b], "wsl")
        tt("dve", v3(TMP1.t[:], 8), v3(TMP1.t[:], 8),
           cst.t[:, C_TRIL:C_TRIL + 128].unsqueeze(1).to_broadcast([128, 8, 128]), ALU.mult, [TMP1.b, cst.b], [TMP1.b])
        for h in range(8):
            tr(PW.t[:, h * 128:(h + 1) * 128], TMP1.t[:, h * 128:(h + 1) * 128], ident, [TMP1.b, cst.b], [PW.b])
        cp("dve", wsT.t[:], PW.t[:], [PW.b], [wsT.b])
        for h in range(8):
            mm(PO.t[0:8, h * 128:(h + 1) * 128], TMP1.t[0:8, h * 128:h * 128 + 8], cst.t[0:8, C_R:C_R + 128],
               True, True, [TMP1.b, cst.b], [PO.b])
        cp("dve", Xs.t[0:8, :], PO.t[0:8, :], [PO.b], [Xs.b])
        for h in range(8):
            mm(PW.t[:, h * 128:(h + 1) * 128], cst.t[0:8, C_R:C_R + 128], Xs.t[0:8, h * 128:(h + 1) * 128],
               True, True, [Xs.b, cst.b], [PW.b])
        tt("dve", v3(wsS.t[:], 8), v3(PW.t[:], 8),
           cst.t[:, C_BD:C_BD + 128].unsqueeze(1).to_broadcast([128, 8, 128]), ALU.mult, [PW.b, cst.b], [wsS.b])
        dma("sp", stg.t[0:1, 0:D], b_s.rearrange("h t -> (h t)").unsqueeze(0), [], [stg.b], "bs1")
        dma("sp", stg.t[32:33, 0:D].rearrange("p (h s t) -> p h s t", h=8, s=16),
            b_s[:, 0:8].unsqueeze(0).unsqueeze(2).to_broadcast([1, 8, 16, 8]), [], [stg.b], "bsS")
        P.op("dve", lambda e: e.memset(stg.t[0:1, D:D + 256], 1.0), [], [stg.b])
        P.op("dve", lambda e: e.memset(stg.t[32:33, D:D + 256], 1.0), [], [stg.b])
        for br_, bo_ in ((0, 0), (32, 8)):
            for hh_ in range(8):
                mm(PX.t[:, bo_ + hh_:bo_ + hh_ + 1], stg.t[br_:br_ + 1, hh_ * 128:(hh_ + 1) * 128], stg.t[br_:br_ + 1, D:D + 1],
                   True, True, [stg.b], [PX.b])
        cp("dve", bsT.t[:], PX.t[:, 0:16], [PX.b], [bsT.b])

        class Task:
            def __init__(self, gen):
                self.gen = gen
                self.done = False

        tasks = []

        def spawn(gen):
            t = Task(gen)
            tasks.append(t)
            return t

        def run_all():
            while tasks:
                for t in list(tasks):
                    try:
                        next(t.gen)
                    except StopIteration:
                        t.done = True
                        tasks.remove(t)

        def join(tl):
            while not all(t.done for t in tl):
                yield

        _A = [("glu", 2), ("glu", 3), ("val", 0), ("val", 1), ("gate", 4), ("gate", 5)]
        _B = [("f", 8), ("f", 9), ("q", 6), ("q", 7), ("i", 10), ("i", 11), ("bg", 12), ("bg", 13)]
        L0_ORDER = _A + _B if L0_AB else _B + _A
        L1_ORDER = [("v", g) for g in range(4, 8)] + [("gate", g) for g in range(8, 12)] + [("u", g) for g in range(0, 4)]
        wq = []
        n_super = SEQ // (128 * NSUB) + 1
        for _ in range(n_super):
            wq += [(w_in_ab, g * 512, 512, 8) for _, g in L0_ORDER]
            wq += [(w_out_ab, g * 256, 256, 16) for g in range(4)]
            wq += [(w_in_c, g * 512, 512, 8) for _, g in L1_ORDER]
            wq += [(w_out_c, g * 256, 256, 16) for g in range(4)]
        wst = {"prod": 0, "cons": 0}

        scr = {}
        scrB = {}
        for nm, src in (("wb_in_ab", w_in_ab), ("wb_out_ab", w_out_ab), ("wb_in_c", w_in_c), ("wb_out_c", w_out_c)):
            scr[id(src)] = nc.dram_tensor(nm, list(src.shape), BF16, kind="Internal").ap()

        def w_fill():
            while wst["prod"] < len(wq) and wst["prod"] < wst["cons"] + NSLOT:
                n = wst["prod"]
                src, c0, ncols, nk = wq[n]
                slot = wslot[n % NSLOT]
                view = slot.t[:, 0:nk * ncols].rearrange("p (k n) -> p k n", k=nk)
                key = (id(src), c0)
                if key not in scrB:
                    scrB[key] = Buf("scr%d_%d" % (len(scrB), c0))
                    dma("pool", view, src[:, c0:c0 + ncols].rearrange("(k p) n -> p k n", p=128), [], [slot.b],
                        "w%d" % (n % NSLOT))
                    dma("sp", scr[id(src)][:, c0:c0 + ncols].rearrange("(k p) n -> p k n", p=128), view, [slot.b], [scrB[key]],
                        "cv%d" % len(scrB))
                else:
                    dma(W_QUEUE, view, scr[id(src)][:, c0:c0 + ncols].rearrange("(k p) n -> p k n", p=128),
                        [scrB[key]], [slot.b], ("w%d" if W_QUEUE == "pool" else "v%d") % (n % NSLOT))
                wst["prod"] += 1

        def w_get(src, c0):
            n = wst["cons"]
            assert wq[n][0] is src and wq[n][1] == c0, (n, c0)
            w_fill()
            _, _, ncols, nk = wq[n]
            slot = wslot[n % NSLOT]
            return slot, slot.t[:, 0:nk * ncols].rearrange("p (k n) -> p k n", k=nk)

        def w_done():
            wst["cons"] += 1
            w_fill()

        pzc = {"n": 0}

        def next_pz():
            pzc["n"] += 1
            return PZ[pzc["n"] % 2]

        def rms_stats(j, src):
            P.op("dve", lambda e: e.memset(rt[j].t[:, 0:1], 0.0), [], [rt[j].b])
            act(xnb[j].t[:], src.t[:], AF.Square, [src.b, rt[j].b], [xnb[j].b, rt[j].b], accum_out=rt[j].t[:, 0:1])
            ts("dve", rt[j].t[:, 2:3], rt[j].t[:, 0:1], 1.0 / D, EPS, ALU.mult, ALU.add, [rt[j].b], [rt[j].b])
            rsqrt(rt[j].t[:, 1:2], rt[j].t[:, 2:3], [rt[j].b], [rt[j].b])

        def t_norm(j, gcol, xsrc=None, r0=0):
            if xsrc is not None:
                dma("sp", xs[j].t[:], xsrc[r0:r0 + 128, :], [], [xs[j].b], "xs%d" % j)
            rms_stats(j, xs[j])
            yield
            tsm("dve", xnb[j].t[:], xs[j].t[:], rt[j].t[:, 1:2], [xs[j].b, rt[j].b], [xnb[j].b])
            yield
            for kc in range(8):
                tr(PT.t[:, kc * 128:(kc + 1) * 128], xnb[j].t[:, kc * 128:(kc + 1) * 128], identb.t[:],
                   [xnb[j].b, identb.b], [PT.b])
            tt("dve", v3(xnT[j].t[:], 8), v3(PT.t[:], 8), cparv[:, :, gcol:gcol + 1].to_broadcast([128, 8, 128]),
               ALU.mult, [PT.b, cpar.b], [xnT[j].b])
            yield

        def proj(j, wview, nk, lhs, ncols, lhs_buf, slot):
            pz = next_pz()
            lv = v3(lhs.t[:, 0:nk * 128], nk)
            for kc in range(nk):
                mm(pz.t[:, 0:ncols], lv[:, kc, :], wview[:, kc, :], kc == 0, kc == nk - 1, [lhs_buf, slot.b], [pz.b])
            return pz

        def transpose8(src_ap, src_buf):
            for kc in range(8):
                tr(PT.t[:, kc * 128:(kc + 1) * 128], src_ap[:, kc * 128:(kc + 1) * 128], identb.t[:],
                   [src_buf, identb.b], [PT.b])

        def t_out_proj(nsub, wsrc):
            for g in range(4):
                slot, wv = w_get(wsrc, g * 256)
                for j in range(nsub):
                    pz = proj(j, wv, 16, cat[j], 256, cat[j].b, slot)
                    tt("dve", xs[j].t[:, g * 256:(g + 1) * 256], xs[j].t[:, g * 256:(g + 1) * 256], pz.t[:, 0:256],
                       ALU.add, [xs[j].b, pz.b], [xs[j].b])
                    yield
                w_done()

        def supertile(kind, nsub, tok0, is_last_prompt):
            sample = kind == "s"
            xsrc = xsm if sample else xp
            L = 8 if sample else 128 * nsub
            Lt = 128 * nsub
            nseg = 16 if sample else 1
            segw = 38 if sample else 30 + L
            cbv = cb.t[:, 0:8 * nseg * segw].rearrange("p (c s l) -> p c s l", c=8, s=nseg)
            UC, MK, UX = (C_UCS, C_MKS, C_UXS) if sample else (C_UCP, C_MKP, C_UXP)
            nch = 16 if sample else 2
            nc3 = 3 * nch
            chunks = [(8 * c, 8 * c + 8) for c in range(16)] if sample else [(0, 64), (64, 128)]

            def t_hist():
                for rt4 in range(4):
                    dma("sp", TMP2.t[0:120, :], sconv[rt4 * 120:(rt4 + 1) * 120, :], [], [TMP2.b], "sch")
                    for ch in range(8):
                        tr(PW.t[:, ch * 128:ch * 128 + 120], TMP2.t[0:120, ch * 128:(ch + 1) * 128],
                           cst.t[0:120, C_ID:C_ID + 120], [TMP2.b, cst.b], [PW.b])
                    cp("dve", cbv[:, :, rt4 * 4:(rt4 + 1) * 4, 0:30],
                       v3(PW.t[:], 8)[:, :, 0:120].rearrange("p c (s l) -> p c s l", s=4), [PW.b], [cb.b])
                    yield
                dma("sp", conv_s[:, 0:22, :], sconv.rearrange("(s r) c -> s r c", r=30)[:, 8:30, :], [], [],
                    "cs0", is_output=True)

            def t_hgrn():
                for j in range(nsub):
                    kt = KQ[j].t[:, 0:1024]; qt = KQ[j].t[:, 1024:2048]
                    vv = VS[j].t[:, 0:1024]; sbg = VS[j].t[:, 1024:2048]
                    sm = SMG[j].t[:, 0:1024]
                    qT = QKT[j].t[:, 0:1024]; kT = QKT[j].t[:, 1024:2048]
                    transpose8(qt, KQ[j].b)
                    cp("act", qT, PT.t[:], [PT.b], [QKT[j].b])
                    yield
                    transpose8(kt, KQ[j].b)
                    cp("act", kT, PT.t[:], [PT.b], [QKT[j].b])
                    yield
                    for h in range(8):
                        hs = slice(h * 128, (h + 1) * 128)
                        mm(PW.t[:, hs], kT[:, hs], qT[:, hs], True, True, [QKT[j].b], [PW.b])
                    tt("dve", v3(sm, 8), v3(PW.t[:], 8), cst.t[:, MK:MK + 128].unsqueeze(1).to_broadcast([128, 8, 128]),
                       ALU.mult, [PW.b, cst.b], [SMG[j].b])
                    yield
                    for h in range(8):
                        hs = slice(h * 128, (h + 1) * 128)
                        mm(PO.t[:, hs], vv[:, hs], sm[:, hs], h % 4 == 0, False, [VS[j].b, SMG[j].b], [PO.b])
                    yield
                    scv = v3(scal[j].t[:, 0:8 * nc3], 8)
                    for ci, (c0, c1) in enumerate(chunks):
                        if sample:
                            Ss = Sld[ci % 2]
                            dma("sp", v3(Ss.t[:], 8), shg[ci].rearrange("h d v -> d h v"), [], [Ss.b], "sld%d" % (ci % 2))
                        else:
                            Ss = Sst
                        bcol = lambda k: scv[:, :, 3 * ci + k:3 * ci + k + 1].to_broadcast([128, 8, 128])
                        tt("dve", v3(Sp.t[:], 8), v3(Ss.t[:], 8), bcol(0), ALU.mult, [Ss.b, scal[j].b], [Sp.b])
                        if sample:
                            tsm("dve", VM.t[:], vv, cst.t[:, C_CMS + ci:C_CMS + ci + 1], [VS[j].b, cst.b], [VM.b])
                        yield
                        for h in range(8):
                            hs = slice(h * 128, (h + 1) * 128)
                            if sample:
                                mm(PW.t[:, hs], kt[:, hs], VM.t[:, hs], True, True, [KQ[j].b, VM.b], [PW.b])
                            else:
                                mm(PW.t[:, hs], kt[c0:c1, hs], vv[c0:c1, hs], True, True, [KQ[j].b, VS[j].b], [PW.b])
                        for h in range(8):
                            hs = slice(h * 128, (h + 1) * 128)
                            mm(PO.t[:, h * 128 + c0:h * 128 + c1], Sp.t[:, hs], qT[:, h * 128 + c0:h * 128 + c1],
                               False, (ci == len(chunks) - 1) and (h % 4 == 3), [Sp.b, QKT[j].b], [PO.b])
                        tt("dve", v3(TMP1.t[:], 8), v3(PW.t[:], 8), bcol(1), ALU.mult, [PW.b, scal[j].b], [TMP1.b])
                        tt("dve", v3(Ss.t[:], 8), v3(Ss.t[:], 8), bcol(2), ALU.mult, [Ss.b, scal[j].b], [Ss.b])
                        tt("dve", Ss.t[:], Ss.t[:], TMP1.t[:], ALU.add, [Ss.b, TMP1.b], [Ss.b])
                        if sample:
                            dma("sp", hg_s[ci].rearrange("h d v -> d h v"), v3(Ss.t[:], 8), [Ss.b], [],
                                "sst%d" % (ci % 2), is_output=True)
                        yield
                    if is_last_prompt and j == nsub - 1:
                        dma("sp", hg_p.rearrange("h d v -> d h v"), v3(Sst.t[:], 8), [Sst.b], [], "hgp", is_output=True)
                    act(osq.t[:], PO.t[:], AF.Square, [PO.b], [osq.b])
                    yield
                    for h2 in range(2):
                        mm(PW.t[:, h2 * 512:(h2 + 1) * 512], onesb.t[:], osq.t[:, h2 * 512:(h2 + 1) * 512], True, True,
                           [onesb.b, osq.b], [PW.b])
                    ts("dve", TMP1.t[:], PW.t[:], 1.0 / 128, EPS, ALU.mult, ALU.add, [PW.b], [TMP1.b])
                    rsqrt(TMP1.t[:], TMP1.t[:], [TMP1.b], [TMP1.b])
                    yield
                    transpose8(sbg, VS[j].b)
                    tt("dve", v3(GS.t[:], 8), v3(PT.t[:], 8), gon.t[:, 0:8].unsqueeze(2).to_broadcast([128, 8, 128]), ALU.mult,
                       [PT.b, gon.b], [GS.b])
                    yield
                    tt("dve", TMP2.t[:], PO.t[:], TMP1.t[:], ALU.mult, [PO.b, TMP1.b], [TMP2.b])
                    tt(CAT_ENG, cat[j].t[:, 1024:2048], TMP2.t[:], GS.t[:], ALU.mult, [TMP2.b, GS.b], [cat[j].b])
                    yield

            def t_conv():
                for j in range(nsub):
                    for ch in range(8):
                        tr(PX.t[:, (ch % 4) * 128:(ch % 4 + 1) * 128], A[j].t[:, ch * 128:(ch + 1) * 128], ident,
                           [A[j].b, cst.b], [PX.b])
                        if ch % 4 == 3:
                            c4 = ch // 4
                            if sample:
                                cp("act", cbv[:, 4 * c4:4 * c4 + 4, :, 30:38],
                                   PX.t[:].rearrange("p (c s l) -> p c s l", c=4, s=16), [PX.b], [cb.b])
                            else:
                                cp("act", cbv[:, 4 * c4:4 * c4 + 4, 0, 30 + j * 128:30 + (j + 1) * 128], v3(PX.t[:], 4),
                                   [PX.b], [cb.b])
                            yield
                    if sample:
                        for sq in range(16):
                            dma("sp", conv_s[sq, 22:30, :], A[j].t[8 * sq:8 * sq + 8, :], [A[j].b], [], "cs1", is_output=True)
                    elif is_last_prompt and j == nsub - 1:
                        dma("sp", conv_p, A[j].t[98:128, :], [A[j].b], [], "cvp", is_output=True)
                ybv = v3(yb.t[:, 0:8 * Lt], 8)

                def win(ch, k):
                    return cbv[:, ch, :, k:k + L] if sample else cbv[:, ch, 0, k:k + L]

                def accv(ch):
                    return ybv[:, ch, :].rearrange("p (s l) -> p s l", s=16) if sample else ybv[:, ch, :]
                ysqv = v3(ysq.t[:, 0:8 * Lt], 8)
                for ch in range(8):
                    db = dgb[ch % 2]
                    dB = dgB[ch % 2]
                    dv = db.t[:].rearrange("p (k m) -> p k m", k=31)
                    for q, (k0, k1, eng) in enumerate(((0, DG_SPLIT, "dve"), (DG_SPLIT, CONV_PE_TAPS, "pool"))):
                        tt(eng, dv[:, k0:k1, :], ident.unsqueeze(1).to_broadcast([128, k1 - k0, 128]),
                           cparv[:, ch, k0:k1].unsqueeze(2).to_broadcast([128, k1 - k0, 128]), ALU.mult,
                           [cst.b, cpar.b], [dB[q]])
                    yield
                    NP = CONV_PE_TAPS
                    for k in range(NP):
                        if sample:
                            mm(PX.t[:, 0:Lt].rearrange("p (s l) -> p s l", s=16), dv[:, k, :], cbv[:, ch, :, k:k + L],
                               k == 0, k == NP - 1, [dB[0], dB[1], cb.b], [PX.b])
                        else:
                            mm(PX.t[:, 0:Lt], dv[:, k, :], cbv[:, ch, 0, k:k + L], k == 0, k == NP - 1, [dB[0], dB[1], cb.b], [PX.b])
                    bias = cparv[:, ch, 31:32]
                    act(ybv[:, ch, :], PX.t[:, 0:Lt], AF.Identity, [PX.b, cpar.b], [ybB[ch]], bias=bias)
                    if NP == 31:
                        act(ysqv[:, ch, :], PX.t[:, 0:Lt], AF.Square, [PX.b, cpar.b], [ysq.b], bias=bias)
                    yield
                    for k in range(NP, 31):
                        av = ybv[:, ch, :].rearrange("p (s l) -> p s l", s=16) if sample else ybv[:, ch, :]
                        wv_ = cbv[:, ch, :, k:k + L] if sample else cbv[:, ch, 0, k:k + L]
                        P.op("dve", (lambda av, wv_, ch, k: lambda e: e.scalar_tensor_tensor(
                            out=av, in0=wv_, scalar=cparv[:, ch, k:k + 1], in1=av, op0=ALU.mult, op1=ALU.add))(av, wv_, ch, k),
                             [cb.b, cpar.b, ybB[ch]], [ybB[ch]], cost=fsz(av) / 0.96 + 90)
                        if (k - NP) % 4 == 3:
                            yield
                    if NP < 31:
                        act(ysqv[:, ch, :], ybv[:, ch, :], AF.Square, [ybB[ch]], [ysq.b])
                        yield
                yield
                for ch in range(8):
                    mm(PW.t[:, 0:Lt], onesmf.t[:], ybv[:, ch, :], ch == 0, ch == 7, [onesmf.b, ybB[ch]], [PW.b])
                for ch in range(8):
                    mm(PW.t[:, 512:512 + Lt], onesm.t[:], v3(ysq.t[:, 0:8 * Lt], 8)[:, ch, :], ch == 0, ch == 7,
                       [onesm.b, ysq.b], [PW.b])
                cp("act", MEAN.t[:, 0:Lt], PW.t[:, 0:Lt], [PW.b], [MEAN.b])
                tt("dve", VAR.t[:, 0:Lt], MEAN.t[:, 0:Lt], MEAN.t[:, 0:Lt], ALU.mult, [MEAN.b], [VAR.b])
                tt("dve", VAR.t[:, 0:Lt], PW.t[:, 512:512 + Lt], VAR.t[:, 0:Lt], ALU.subtract, [PW.b, VAR.b], [VAR.b])
                ts("dve", VAR.t[:, 0:Lt], VAR.t[:, 0:Lt], 1.0, EPS, ALU.mult, ALU.add, [VAR.b], [VAR.b])
                rsqrt(RSTD.t[:, 0:Lt], VAR.t[:, 0:Lt], [VAR.b], [RSTD.b])
                yield
                tt("dve", ybv, ybv, MEAN.t[:, 0:Lt].unsqueeze(1).to_broadcast([128, 8, Lt]), ALU.subtract,
                   ybB + [MEAN.b], ybB)
                tt("dve", ybv, ybv, RSTD.t[:, 0:Lt].unsqueeze(1).to_broadcast([128, 8, Lt]), ALU.mult, ybB + [RSTD.b], ybB)
                yield
                for ch in range(8):
                    act(ybv[:, ch, :], ybv[:, ch, :], AF.Silu, [ybB[ch], cpar.b], [ybB[ch]],
                        scale=cparv[:, ch, 32:33], bias=cparv[:, ch, 33:34])
                yield
                for j in range(nsub):
                    transpose8(SMG[j].t[:, 1024:2048], SMG[j].b)
                    tt("dve", v3(cat[j].t[:, 0:1024], 8), ybv[:, :, j * 128:(j + 1) * 128], v3(PT.t[:], 8), ALU.mult,
                       ybB + [PT.b], [cat[j].b])
                    yield
                if not sample:
                    cp("act", cbv[:, :, 0, 0:30], cbv[:, :, 0, L:L + 30], [cb.b], [cb.b])

            def t_vln(j):
                VG = FE[j]
                r = rt[j]
                act(KQ[j].t[:], VG.t[:], AF.Square, [VG.b, r.b], [KQ[j].b, r.b], accum_out=r.t[:, 8:9])
                P.op("dve", lambda e: e.tensor_reduce(out=r.t[:, 9:10], in_=r.t[:, 4:8], axis=AX.X, op=ALU.add), [r.b], [r.b])
                ts("dve", r.t[:, 9:10], r.t[:, 9:10], 1.0 / 2048, 0.0, ALU.mult, ALU.add, [r.b], [r.b])
                tt("dve", r.t[:, 10:11], r.t[:, 9:10], r.t[:, 9:10], ALU.mult, [r.b], [r.b])
                ts("dve", r.t[:, 11:12], r.t[:, 8:9], 1.0 / 2048, EPS, ALU.mult, ALU.add, [r.b], [r.b])
                tt("dve", r.t[:, 11:12], r.t[:, 11:12], r.t[:, 10:11], ALU.subtract, [r.b], [r.b])
                rsqrt(r.t[:, 12:13], r.t[:, 11:12], [r.b], [r.b])
                tt("dve", r.t[:, 13:14], r.t[:, 9:10], r.t[:, 12:13], ALU.mult, [r.b], [r.b])
                ts("dve", r.t[:, 13:14], r.t[:, 13:14], -1.0, 0.0, ALU.mult, ALU.add, [r.b], [r.b])
                yield
                act(VG.t[:], VG.t[:], AF.Identity, [VG.b, r.b], [VG.b], scale=r.t[:, 12:13], bias=r.t[:, 13:14])
                yield
                if VLN_SPLIT:
                    for eng_, sl_ in (("pool", slice(0, VLN_SPLIT)), ("dve", slice(VLN_SPLIT, 2048))):
                        tt(eng_, VG.t[:, sl_], VG.t[:, sl_], lncg_bc.t[:, sl_], ALU.mult, [VG.b, lncg_bc.b], [VG.b])
                    yield
                    for eng_, sl_ in (("pool", slice(0, VLN_SPLIT)), ("dve", slice(VLN_SPLIT, 2048))):
                        tt(eng_, VG.t[:, sl_], VG.t[:, sl_], lncb_bc.t[:, sl_], ALU.add, [VG.b, lncb_bc.b], [VG.b])
                else:
                    tt("pool", VG.t[:], VG.t[:], lncg_bc.t[:], ALU.mult, [VG.b, lncg_bc.b], [VG.b])
                    yield
                    tt("dve", VG.t[:], VG.t[:], lncb_bc.t[:], ALU.add, [VG.b, lncb_bc.b], [VG.b])
                yield
                cp("act", VS[j].t[:], VG.t[:], [VG.b], [VS[j].b])
                if sample:
                    dma("sp", gv_s, VG.t[:], [VG.b], [], "gvs", is_output=True)
                elif is_last_prompt and j == nsub - 1:
                    dma("sp", gv_p, VG.t[:], [VG.b], [], "gvp", is_output=True)

            def t_mix():
                wsX = wsS if sample else wsT
                br = 32 if sample else 0
                for j in range(nsub):
                    for half in range(2):
                        PH = PW if half == 0 else PO
                        for hl in range(4):
                            h = half * 4 + hl
                            mm(PH.t[:, hl * 256:(hl + 1) * 256], wsX.t[:, h * 128:(h + 1) * 128], VS[j].t[:, h * 256:(h + 1) * 256],
                               True, True, [wsX.b, VS[j].b], [PH.b])
                        bo = 8 if sample else 0
                        for hl in range(4):
                            h = half * 4 + hl
                            P.op("dve", (lambda j, h, hl, PH, bo: lambda e: e.scalar_tensor_tensor(
                                out=KQ[j].t[:, h * 256:(h + 1) * 256], in0=PH.t[:, hl * 256:(hl + 1) * 256],
                                scalar=bsT.t[:, bo + h:bo + h + 1], in1=SMG[j].t[:, h * 256:(h + 1) * 256],
                                op0=ALU.add, op1=ALU.mult))(j, h, hl, PH, bo),
                                 [PH.b, SMG[j].b, bsT.b], [KQ[j].b], cost=360.0)
                        yield
                    for half in range(2):
                        transpose8(KQ[j].t[:, half * 1024:(half + 1) * 1024], KQ[j].b)
                        cp("act", cat[j].t[:, half * 1024:(half + 1) * 1024], PT.t[:], [PT.b], [cat[j].b])
                        yield

            tl = [spawn(t_norm(j, 34, xsrc, tok0 + j * 128)) for j in range(nsub)]
            if sample:
                tl.append(spawn(t_hist()))
            yield from join(tl)
            mix_tasks = []
            for kindg, g in L0_ORDER:
                hh = g % 2
                cs = slice(hh * 512, (hh + 1) * 512)
                slot, wv = w_get(w_in_ab, g * 512)
                for j in range(nsub):
                    pz = proj(j, wv, 8, xnT[j], 512, xnT[j].b, slot)
                    F = FE[j].t[:, 0:1024]; E1 = FE[j].t[:, 1024:2048]
                    if kindg == "f":
                        act(F[:, cs], pz.t[:], AF.Sigmoid, [pz.b], [FE[j].b])
                        if hh == 1:
                            yield
                            tt("dve", F, F, oml_bc.t[:], ALU.mult, [FE[j].b, oml_bc.b], [FE[j].b])
                            tt("dve", F, F, lb_bc.t[:], ALU.add, [FE[j].b, lb_bc.b], [FE[j].b])
                            ts("dve", kk[j].t[:], F, -1.0, 1.0, ALU.mult, ALU.add, [FE[j].b], [kk[j].b])
                            act(F, F, AF.Ln, [FE[j].b], [FE[j].b])
                            yield
                            for h2 in range(2):
                                mm(PW.t[:, h2 * 512:(h2 + 1) * 512], cst.t[:, UC:UC + 128], F[:, h2 * 512:(h2 + 1) * 512],
                                   True, True, [cst.b, FE[j].b], [PW.b])
                            for h in range(8):
                                mm(PX.t[:, h * nc3:(h + 1) * nc3], F[:, h * 128:(h + 1) * 128], cst.t[:, UX:UX + nc3],
                                   True, True, [cst.b, FE[j].b], [PX.b])
                            act(E1, PW.t[:], AF.Exp, [PW.b], [FE[j].b])
                            act(TMP1.t[:], PW.t[:], AF.Exp, [PW.b], [TMP1.b], scale=-1.0)
                            act(scal[j].t[:, 0:8 * nc3], PX.t[:, 0:8 * nc3], AF.Exp, [PX.b], [scal[j].b])
                            tt("dve", KQ[j].t[:, 0:1024], kk[j].t[:], TMP1.t[:], ALU.mult, [kk[j].b, TMP1.b], [KQ[j].b])
                    elif kindg == "q":
                        tt("dve", KQ[j].t[:, 1024 + hh * 512:1024 + (hh + 1) * 512], pz.t[:], E1[:, cs], ALU.mult,
                           [pz.b, FE[j].b], [KQ[j].b])
                    elif kindg == "i":
                        act(VS[j].t[:, cs], pz.t[:], AF.Silu, [pz.b], [VS[j].b])
                    elif kindg == "bg":
                        act(VS[j].t[:, 1024 + hh * 512:1024 + (hh + 1) * 512], pz.t[:], AF.Silu, [pz.b], [VS[j].b])
                    elif kindg == "glu":
                        act(A[j].t[:, cs], pz.t[:], AF.Sigmoid, [pz.b], [A[j].b])
                    elif kindg == "val":
                        tt("dve", A[j].t[:, cs], pz.t[:], A[j].t[:, cs], ALU.mult, [pz.b, A[j].b], [A[j].b])
                    elif kindg == "gate":
                        act(SMG[j].t[:, 1024 + hh * 512:1024 + (hh + 1) * 512], pz.t[:], AF.Silu, [pz.b], [SMG[j].b])
                    yield
                w_done()
                if (kindg, g) == ("gate", 5):
                    mix_tasks.append(spawn(t_conv()))
                if (kindg, g) == ("bg", 13):
                    mix_tasks.append(spawn(t_hgrn()))
            yield from join(mix_tasks)
            yield from t_out_proj(nsub, w_out_ab)
            yield from join([spawn(t_norm(j, 35)) for j in range(nsub)])
            vtasks = []
            for kindg, g in L1_ORDER:
                gi = g % 4
                cs = slice(gi * 512, (gi + 1) * 512)
                slot, wv = w_get(w_in_c, g * 512)
                for j in range(nsub):
                    pz = proj(j, wv, 8, xnT[j], 512, xnT[j].b, slot)
                    VG = FE[j]; SG = QKT[j]; UG = SMG[j]
                    if kindg == "v":
                        if gi == 0:
                            P.op("dve", lambda e, j=j: e.memset(rt[j].t[:, 4:12], 0.0), [], [rt[j].b])
                        act(VG.t[:, cs], pz.t[:], AF.Gelu_apprx_tanh, [pz.b, rt[j].b], [VG.b, rt[j].b],
                            accum_out=rt[j].t[:, 4 + gi:5 + gi])
                        if gi == 3:
                            vtasks.append(spawn(t_vln(j)))
                    elif kindg == "gate":
                        act(SG.t[:, cs], pz.t[:], AF.Silu, [pz.b], [SG.b])
                    else:
                        act(UG.t[:, cs], pz.t[:], AF.Gelu_apprx_tanh, [pz.b], [UG.b])
                        if gi == 3:
                            tt("dve", UG.t[:], UG.t[:], SG.t[:], ALU.mult, [UG.b, SG.b], [UG.b])
                    yield
                w_done()
            yield from join(vtasks)
            yield from t_mix()
            yield from t_out_proj(nsub, w_out_c)
            for j in range(nsub):
                rms_stats(j, xs[j])
                P.op("dve", lambda e, j=j: e.scalar_tensor_tensor(out=TMP2.t[:], in0=xs[j].t[:], scalar=rt[j].t[:, 1:2],
                                                                  in1=fin_bc.t[:], op0=ALU.mult, op1=ALU.mult),
                     [xs[j].b, rt[j].b, fin_bc.b], [TMP2.b])
                r0 = tok0 + j * 128
                dst = y_s if sample else y_p
                dma("sp", dst[r0:r0 + 128, :], TMP2.t[:], [TMP2.b], [], "yout", is_output=True)
                yield

        def main():
            nst = SEQ // (128 * NSUB)
            for s_i in range(nst):
                yield from supertile("p", NSUB, s_i * 128 * NSUB, s_i == nst - 1)
            yield from supertile("s", 1, 0, False)

        w_fill()
        spawn(main())
        run_all()
        assert wst["cons"] == len(wq), (wst, len(wq))
        if LIST_SCHED:
            P.schedule(SCHED_WINDOW)
        P.emit(nc, st)
    return nc


_CACHE = {}


def kernel(**inputs):
    f = lambda a: np.ascontiguousarray(np.asarray(a, dtype=np.float32))
    x_prompt = f(inputs["x_prompt"]); x_sample = f(inputs["x_sample"])
    state_conv = f(inputs["state_conv"]); state_hgrn = f(inputs["state_hgrn"])
    if "nc" not in _CACHE:
        _CACHE["nc"] = build_program()
    nc = _CACHE["nc"]
    shared = {
        "norm_ab": f(inputs["norm_ab"]).reshape(1, D), "w_in_ab": f(inputs["w_in_ab"])[0],
        "conv_w": f(inputs["conv_w"])[0], "conv_b": f(inputs["conv_b"]).reshape(1, D),
        "ln_a_g": f(inputs["ln_a_g"]).reshape(1, D), "ln_a_b": f(inputs["ln_a_b"]).reshape(1, D),
        "lb_logits": f(inputs["lb_logits"]), "onorm_b": f(inputs["onorm_b"])[0],
        "w_out_ab": f(inputs["w_out_ab"])[0], "norm_c": f(inputs["norm_c"]).reshape(1, D),
        "w_in_c": f(inputs["w_in_c"])[0], "ln_c_g": f(inputs["ln_c_g"]).reshape(1, 2048),
        "ln_c_b": f(inputs["ln_c_b"]).reshape(1, 2048), "w_s": f(inputs["w_s"])[0], "b_s": f(inputs["b_s"])[0],
        "w_out_c": f(inputs["w_out_c"])[0], "final_norm": f(inputs["final_norm"]).reshape(1, D),
        "cst": make_consts(),
    }
    in_maps = []
    for c in range(NCORES):
        m = dict(shared)
        m["xp"] = x_prompt[c]
        m["xsm"] = x_sample[16 * c:16 * (c + 1)].reshape(128, D)
        m["sconv"] = state_conv[0, 16 * c:16 * (c + 1)].reshape(480, D)
        m["shg"] = state_hgrn[0, 16 * c:16 * (c + 1)]
        in_maps.append(m)
    res = run_bass_kernel_spmd(nc, in_maps, core_ids=list(range(NCORES)))
    R = res.results
    y_prompt = np.stack([R[c]["y_p"] for c in range(NCORES)], 0)
    y_sample = np.concatenate([R[c]["y_s"].reshape(16, 8, D) for c in range(NCORES)], 0)
    conv_prompt = np.stack([R[c]["conv_p"] for c in range(NCORES)], 0)[None]
    hgrn_prompt = np.stack([R[c]["hg_p"] for c in range(NCORES)], 0)[None]
    gv_prompt = np.stack([R[c]["gv_p"] for c in range(NCORES)], 0)[None]
    conv_sample = np.concatenate([R[c]["conv_s"] for c in range(NCORES)], 0)[None]
    hgrn_sample = np.concatenate([R[c]["hg_s"] for c in range(NCORES)], 0)[None]
    gv_sample = np.concatenate([R[c]["gv_s"].reshape(16, 8, 2048) for c in range(NCORES)], 0)[None]
    return tuple(np.ascontiguousarray(a, dtype=np.float32) for a in
                 (y_prompt, y_sample, conv_prompt, hgrn_prompt, gv_prompt, conv_sample, hgrn_sample, gv_sample))
```

```python
import contextlib
import numpy as np
import concourse.bass as bass
import concourse.mybir as mybir
from concourse.bass_utils import run_bass_kernel_spmd

F32 = mybir.dt.float32
BF16 = mybir.dt.bfloat16
AF = mybir.ActivationFunctionType
ALU = mybir.AluOpType
AX = mybir.AxisListType

NCORES = 8
D = 1024
SEQ = 2048
NSUB = 2
NARENA = 2
LIST_SCHED = True
ACT_SWITCH_NS = 0.0
ACT_SWITCH_COST = 1300.0
DG_SPLIT = 20
VLN_SPLIT = 640
CAT_ENG = "dve"
W_QUEUE = "sp"
L0_AB = True
CONV_PE_TAPS = 31
SCHED_WINDOW = 140
SAMPLE_ALONE = True
NSLOT = 3
EPS = 1e-6

C_ID, C_TRIL, C_UCP, C_MKP, C_UXP = 0, 128, 256, 384, 512
C_UCS, C_MKS, C_UXS, C_CMS, C_BD, C_R = 520, 648, 776, 824, 840, 968
NCST = 1096


def make_consts():
    c = np.zeros((128, NCST), np.float32)
    i = np.arange(128)
    c[:, C_ID:C_ID + 128] = np.eye(128)
    c[:, C_TRIL:C_TRIL + 128] = (i[None, :] <= i[:, None])

    def fill(clen, ucol, mcol, xcol):
        ch = i // clen
        mid = ch * clen + clen // 2
        same = ch[:, None] == ch[None, :]
        c[:, ucol:ucol + 128] = same * ((i[:, None] <= i[None, :]).astype(np.float32)
                                        - (i[:, None] <= mid[None, :]).astype(np.float32))
        c[:, mcol:mcol + 128] = same * (i[:, None] <= i[None, :])
        for cc in range(128 // clen):
            inc = ch == cc
            c[:, xcol + 3 * cc + 0] = inc * (i <= mid)
            c[:, xcol + 3 * cc + 1] = inc * (i > mid)
            c[:, xcol + 3 * cc + 2] = inc
    fill(64, C_UCP, C_MKP, C_UXP)
    fill(8, C_UCS, C_MKS, C_UXS)
    for cc in range(16):
        c[:, C_CMS + cc] = (i // 8 == cc)
    c[:, C_BD:C_BD + 128] = (i[:, None] // 8 == i[None, :] // 8)
    for a in range(8):
        c[a, C_R:C_R + 128] = (i % 8 == a)
    return c


class Buf:
    __slots__ = ("name", "writer", "readers")

    def __init__(self, name):
        self.name = name
        self.writer = None
        self.readers = []


class Op:
    __slots__ = ("eng", "fn", "deps", "ticket", "has_dep", "dma_key", "idx", "cost", "nbytes", "fin", "seq", "wn", "tbl")

    def __init__(self, eng, fn, deps, dma_key=None):
        self.cost = 100.0
        self.nbytes = 0
        self.tbl = 0
        self.fin = None
        self.seq = 0
        self.eng = eng
        self.fn = fn
        self.deps = deps
        self.ticket = None
        self.has_dep = False
        self.dma_key = dma_key
        self.idx = None


class Prog:
    ENGS = ("pe", "act", "dve", "pool", "sp")

    def __init__(self):
        self.ops = {e: [] for e in self.ENGS}
        self.dma_keys = {}
        self.outs = []

    def op(self, eng, fn, reads=(), writes=(), dma_key=None, is_output=False, cost=100.0, nbytes=0):
        d = set()
        for b in reads:
            if b.writer is not None:
                d.add(b.writer)
        for b in writes:
            if b.writer is not None:
                d.add(b.writer)
            d.update(b.readers)
        o = Op(eng, fn, d, dma_key)
        o.cost = cost
        o.nbytes = nbytes
        o.wn = ",".join(b.name for b in writes) + "<" + ",".join(b.name for b in reads)
        self.nops = getattr(self, "nops", 0) + 1
        o.seq = self.nops
        o.idx = len(self.ops[eng])
        self.ops[eng].append(o)
        for b in reads:
            b.readers.append(o)
        for b in writes:
            b.writer = o
            b.readers = []
        if dma_key is not None:
            self.dma_keys.setdefault(dma_key, eng)
            assert self.dma_keys[dma_key] == eng
        if is_output:
            self.outs.append(o)
        return o

    def schedule(self, window=40):
        lists = {e: list(self.ops[e]) for e in self.ENGS}
        out = {e: [] for e in self.ENGS}
        free = {e: 0.0 for e in self.ENGS}
        hbm_free = 0.0
        cur_tbl = [0]
        remaining = sum(len(v) for v in lists.values())
        while remaining:
            best = None
            for e in self.ENGS:
                L = lists[e]
                for i in range(min(window, len(L))):
                    o = L[i]
                    rdy = 0.0
                    ok = True
                    for dd in o.deps:
                        if dd.fin is None:
                            ok = False
                            break
                        lat = dd.fin if dd.eng == e and dd.dma_key is None else dd.fin + 70.0
                        if lat > rdy:
                            rdy = lat
                    if not ok:
                        continue
                    start = max(free[e], rdy)
                    sw = e == "act" and o.tbl and o.tbl != cur_tbl[0]
                    pen = ACT_SWITCH_NS if sw else 0.0
                    key = (start + pen, o.seq)
                    if best is None or key < best[0]:
                        best = (key, e, i, o, start + (ACT_SWITCH_COST if sw else 0.0))
                    if start <= free[e] and not pen:
                        break
            _, e, i, o, start = best
            lists[e].pop(i)
            out[e].append(o)
            if e == "act" and o.tbl:
                cur_tbl[0] = o.tbl
            if o.dma_key is not None:
                free[e] = start + o.cost
                t0 = max(free[e], hbm_free)
                hbm_free = t0 + o.nbytes / 330.0
                o.fin = hbm_free + 1800.0
            else:
                free[e] = start + o.cost
                o.fin = free[e]
            remaining -= 1
        for e in self.ENGS:
            for k, o in enumerate(out[e]):
                o.idx = k
            self.ops[e] = out[e]
        self.est_ns = max(o.fin for e in self.ENGS for o in self.ops[e])

    def emit(self, nc, stack):
        def skip(dd, o):
            return dd.dma_key is None and o.dma_key is None and dd.eng == "pe" and o.eng == "pe"
        def needed(o):
            last = {}
            out = []
            for dd in o.deps:
                if skip(dd, o):
                    continue
                if dd.dma_key is not None:
                    out.append(dd)
                elif dd.eng not in last or dd.idx > last[dd.eng].idx:
                    last[dd.eng] = dd
            return out + list(last.values())
        for e in self.ENGS:
            for k, o in enumerate(self.ops[e]):
                o.idx = k
        for e in self.ENGS:
            for o in self.ops[e]:
                for dd in needed(o):
                    dd.has_dep = True
        sems = {e: stack.enter_context(nc.semaphore("s_" + e)) for e in ("pe", "act", "dve", "pool")}
        dsems = {k: stack.enter_context(nc.semaphore("d_" + k)) for k in self.dma_keys}
        dcnt = {k: 0 for k in self.dma_keys}
        for e in self.ENGS:
            cnt = 0
            for o in self.ops[e]:
                if o.dma_key is not None:
                    dcnt[o.dma_key] += 16
                    o.ticket = dcnt[o.dma_key]
                elif o.has_dep:
                    cnt += 1
                    o.ticket = cnt
        fw = {}
        for o in self.outs:
            fw[o.dma_key] = max(fw.get(o.dma_key, 0), o.ticket)
        block = stack.enter_context(nc.Block())
        handles = {"pe": "tensor", "act": "scalar", "dve": "vector", "pool": "gpsimd", "sp": "sync"}

        def make_body(e):
            def body(eng):
                waited = {}
                for o in self.ops[e]:
                    for dd in sorted(needed(o), key=lambda z: (z.eng, z.idx)):
                        if dd.dma_key is not None:
                            key, sem = ("d", dd.dma_key), dsems[dd.dma_key]
                        else:
                            key, sem = ("e", dd.eng), sems[dd.eng]
                        if waited.get(key, 0) >= dd.ticket:
                            continue
                        eng.wait_ge(sem, dd.ticket)
                        waited[key] = dd.ticket
                    ins = o.fn(eng)
                    if o.dma_key is not None:
                        ins.then_inc(dsems[o.dma_key], 16)
                    elif o.has_dep:
                        ins.then_inc(sems[e], 1)
                if e == "sp":
                    for k, t in fw.items():
                        eng.wait_ge(dsems[k], t)
            return body

        for e in self.ENGS:
            getattr(block, handles[e])(make_body(e))


class T:
    def __init__(self, t, name, buf=None):
        self.t = t
        self.b = buf if buf is not None else Buf(name)


def build_program():
    nc = bass.Bass("TRN2", target_bir_lowering=False)
    din = lambda n, s: nc.dram_tensor(n, list(s), F32, kind="ExternalInput").ap()
    dout = lambda n, s: nc.dram_tensor(n, list(s), F32, kind="ExternalOutput").ap()
    xp = din("xp", (SEQ, D)); xsm = din("xsm", (128, D))
    sconv = din("sconv", (16 * 30, D)); shg = din("shg", (16, 8, 128, 128))
    norm_ab = din("norm_ab", (1, D)); w_in_ab = din("w_in_ab", (D, 7168))
    conv_w = din("conv_w", (31, D)); conv_b = din("conv_b", (1, D))
    ln_a_g = din("ln_a_g", (1, D)); ln_a_b = din("ln_a_b", (1, D))
    lb_logits = din("lb_logits", (2, D)); onorm_b = din("onorm_b", (8, 128))
    w_out_ab = din("w_out_ab", (2048, D)); norm_c = din("norm_c", (1, D))
    w_in_c = din("w_in_c", (D, 6144)); ln_c_g = din("ln_c_g", (1, 2048)); ln_c_b = din("ln_c_b", (1, 2048))
    w_s = din("w_s", (8, 128, 128)); b_s = din("b_s", (8, 128))
    w_out_c = din("w_out_c", (2048, D)); final_norm = din("final_norm", (1, D))
    cst_d = din("cst", (128, NCST))
    y_p = dout("y_p", (SEQ, D)); y_s = dout("y_s", (128, D))
    conv_p = dout("conv_p", (30, D)); hg_p = dout("hg_p", (8, 128, 128)); gv_p = dout("gv_p", (128, 2048))
    conv_s = dout("conv_s", (16, 30, D)); hg_s = dout("hg_s", (16, 8, 128, 128)); gv_s = dout("gv_s", (128, 2048))

    P = Prog()
    with contextlib.ExitStack() as st:
        def sb(name, cols, dt=F32, parts=128):
            return T(st.enter_context(nc.sbuf_tensor("sb_" + name, [parts, cols], dt)), name)

        def ps(name, cols, dt=F32):
            return T(st.enter_context(nc.psum_tensor("ps_" + name, [128, cols], dt)), name)

        PZ = [ps("PZ0", 512), ps("PZ1", 512)]
        PT = ps("PT", 1024, BF16)
        PW = ps("PW", 1024)
        PO = ps("PO", 1024)
        PX = ps("PX", 512)
        cst = sb("cst", NCST)
        identb = sb("identb", 128, BF16)
        onesm = sb("onesm", 128, BF16)
        onesb = sb("onesb", 128, BF16)
        onesmf = sb("onesmf", 128)
        mh = sb("mh", 1)
        lb_bc = sb("lb_bc", D); oml_bc = sb("oml_bc", D); fin_bc = sb("fin_bc", D)
        lncg_bc = sb("lncg_bc", 2048); lncb_bc = sb("lncb_bc", 2048)
        stg = sb("stg", D + 256, F32, parts=36)
        cpar = sb("cpar", 8 * 36)
        stg2 = sb("stg2", 128, F32, parts=8)
        gon = sb("gon", 8)
        bsT = sb("bsT", 16)
        wsT = sb("wsT", 1024, BF16); wsS = sb("wsS", 1024, BF16)
        wslot = [sb("wslot%d" % i, 8 * 512, BF16) for i in range(NSLOT)]
        xs = [sb("xs%d" % j, D) for j in range(NARENA)]
        xnb = [sb("xnb%d" % j, D, BF16) for j in range(NARENA)]
        xnT = [sb("xnT%d" % j, D, BF16) for j in range(NARENA)]
        rt = [sb("rt%d" % j, 16) for j in range(NARENA)]
        FE = [sb("FE%d" % j, 2048) for j in range(NARENA)]
        kk = xnb
        KQ = [sb("KQ%d" % j, 2048, BF16) for j in range(NARENA)]
        VS = [sb("VS%d" % j, 2048, BF16) for j in range(NARENA)]
        A = [sb("A%d" % j, D) for j in range(NARENA)]
        SMG = [sb("SMG%d" % j, 2048, BF16) for j in range(NARENA)]
        QKT = [sb("QKT%d" % j, 2048, BF16) for j in range(NARENA)]
        cat = [sb("cat%d" % j, 2048, BF16) for j in range(NARENA)]
        scal = [sb("scal%d" % j, 8 * 48) for j in range(NARENA)]
        TMP1 = sb("TMP1", D); TMP2 = sb("TMP2", D)
        Sst = sb("Sst", D); Sp = sb("Sp", D, BF16); osq = sb("osq", D, BF16)
        GS = Sp; VM = osq; Xs = TMP2; bs1 = stg
        SA = (SEQ // 128) % NARENA
        Sld = [A[1], xs[1]]
        LMAX = 128 * NSUB
        yb = sb("yb", 8 * LMAX); ysq = sb("ysq", 8 * LMAX, BF16)
        ybB = [Buf("yb%d" % c) for c in range(8)]
        yb2B = [Buf("yb2_%d" % c) for c in range(8)]
        dgb = [sb("dgb%d" % i, 31 * 128, BF16) for i in range(2)]
        dgB = [[Buf("dgb%d_%d" % (i, q)) for q in range(2)] for i in range(2)]
        MEAN = sb("MEAN", LMAX); VAR = sb("VAR", LMAX); RSTD = sb("RSTD", LMAX)
        CBW = max(30 + LMAX, 16 * 38)
        cb = sb("cb", 8 * CBW, BF16)
        def fsz(ap):
            n = 1
            for s_ in ap.shape[1:]:
                n *= int(s_)
            return n

        def esz(ap):
            return 2 if ap.dtype == BF16 else 4

        def dma(eng, out, in_, reads, writes, key, is_output=False):
            nb = int(out.shape[0]) * fsz(out) * max(esz(out), esz(in_))
            return P.op(eng, lambda e: e.dma_start(out=out, in_=in_), reads, writes, dma_key=key, is_output=is_output,
                        cost=1100.0 if eng == "pool" else 80.0, nbytes=nb)

        ACT_TBL = {AF.Sigmoid: 1, AF.Silu: 2, AF.Ln: 3, AF.Exp: 3, AF.Gelu_apprx_tanh: 4}

        def act(out, in_, func, reads, writes, **kw):
            o = P.op("act", lambda e: e.activation(out=out, in_=in_, func=func, **kw), reads, writes,
                     cost=(fsz(out) + 260) / 1.2)
            o.tbl = ACT_TBL.get(func, 0)
            return o

        def vcost(eng, out):
            n = fsz(out)
            return n * 2.3 + 120 if eng == "pool" else n / 0.96 + 90

        def tt(eng, out, in0, in1, op, reads, writes):
            return P.op(eng, lambda e: e.tensor_tensor(out=out, in0=in0, in1=in1, op=op), reads, writes, cost=vcost(eng, out))

        def ts(eng, out, in0, s1, s2, op0, op1, reads, writes):
            return P.op(eng, lambda e: e.tensor_scalar(out=out, in0=in0, scalar1=s1, scalar2=s2, op0=op0, op1=op1),
                        reads, writes, cost=vcost(eng, out))

        def tsm(eng, out, in0, s1, reads, writes):
            return P.op(eng, lambda e: e.tensor_scalar_mul(out=out, in0=in0, scalar1=s1), reads, writes, cost=vcost(eng, out))

        def cp(eng, out, in_, reads, writes):
            if eng == "act":
                return P.op(eng, lambda e: e.activation(out=out, in_=in_, func=AF.Copy), reads, writes,
                            cost=(fsz(out) + 260) / 1.2)
            return P.op(eng, lambda e: e.tensor_copy(out=out, in_=in_), reads, writes, cost=vcost(eng, out))

        def mm(out, lhsT, rhs, start, stop, reads, writes):
            c = max(fsz(out), 96) / 2.0 + 45
            if lhsT.dtype == F32:
                c *= 3.0
            return P.op("pe", lambda e: e.matmul(out=out, lhsT=lhsT, rhs=rhs, start=start, stop=stop), reads, writes, cost=c)

        def tr(out, in_, ident, reads, writes):
            return P.op("pe", lambda e: e.transpose(out=out, in_=in_, identity=ident), reads, writes,
                        cost=260.0 if in_.dtype == F32 else 110.0)

        def rsqrt(out, in_, reads, writes):
            act(out, in_, AF.Ln, reads, writes)
            return act(out, out, AF.Exp, writes, writes, scale=-0.5)

        def v3(ap, a):
            return ap.rearrange("p (a b) -> p a b", a=a)

        ident = cst.t[:, C_ID:C_ID + 128]

        dma("sp", cst.t[:], cst_d, [], [cst.b], "cst")
        cp("dve", identb.t[:], ident, [cst.b], [identb.b])
        P.op("pool", lambda e: e.memset(onesm.t[:], 1.0 / 1024), [], [onesm.b])
        P.op("pool", lambda e: e.memset(onesb.t[:], 1.0), [], [onesb.b])
        P.op("pool", lambda e: e.memset(onesmf.t[:], 1.0 / 1024), [], [onesmf.b])
        P.op("pool", lambda e: e.memset(mh.t[:], -0.5), [], [mh.b])
        P.op("pool", lambda e: e.memset(Sst.t[:], 0.0), [], [Sst.b])
        P.op("pool", lambda e: e.memset(cb.t[:], 0.0), [], [cb.b])
        dma("sp", fin_bc.t[:], final_norm.to_broadcast([128, D]), [], [fin_bc.b], "bc0")
        dma("sp", lncg_bc.t[:], ln_c_g.to_broadcast([128, 2048]), [], [lncg_bc.b], "bc1")
        dma("sp", lncb_bc.t[:], ln_c_b.to_broadcast([128, 2048]), [], [lncb_bc.b], "bc2")
        dma("sp", lb_bc.t[:], lb_logits[0:1, :].to_broadcast([128, D]), [], [lb_bc.b], "bc3")
        dma("sp", oml_bc.t[:], lb_logits[1:2, :].to_broadcast([128, D]), [], [oml_bc.b], "bc4")
        tt("dve", lb_bc.t[:], lb_bc.t[:], oml_bc.t[:], ALU.subtract, [lb_bc.b, oml_bc.b], [lb_bc.b])
        act(lb_bc.t[:], lb_bc.t[:], AF.Sigmoid, [lb_bc.b], [lb_bc.b])
        ts("dve", oml_bc.t[:], lb_bc.t[:], -1.0, 1.0, ALU.mult, ALU.add, [lb_bc.b], [oml_bc.b])
        dma("sp", stg.t[0:31, 0:D], conv_w, [], [stg.b], "stg")
        dma("sp", stg.t[31:32, 0:D], conv_b, [], [stg.b], "stg")
        dma("sp", stg.t[32:33, 0:D], ln_a_g, [], [stg.b], "stg")
        dma("sp", stg.t[33:34, 0:D], ln_a_b, [], [stg.b], "stg")
        dma("sp", stg.t[34:35, 0:D], norm_ab, [], [stg.b], "stg")
        dma("sp", stg.t[35:36, 0:D], norm_c, [], [stg.b], "stg")
        for ch in range(8):
            tr(PW.t[:, ch * 36:(ch + 1) * 36], stg.t[0:36, ch * 128:(ch + 1) * 128], cst.t[0:36, C_ID:C_ID + 36],
               [stg.b, cst.b], [PW.b])
        cp("dve", cpar.t[:], PW.t[:, 0:8 * 36], [PW.b], [cpar.b])
        cparv = v3(cpar.t[:], 8)
        dma("sp", stg2.t[:], onorm_b, [], [stg2.b], "stg2")
        tr(PX.t[:, 0:8], stg2.t[0:8, :], cst.t[0:8, C_ID:C_ID + 8], [stg2.b, cst.b], [PX.b])
        cp("dve", gon.t[:], PX.t[:, 0:8], [PX.b], [gon.b])
        dma("sp", v3(TMP1.t[:], 8), w_s.rearrange("h t s -> t h s"), [], [TMP1.b], "wsl")
        tt("dve", v3(TMP1.t[:], 8), v3(TMP1.t[:], 8),
           cst.t[:, C_TRIL:C_TRIL + 128].unsqueeze(1).to_broadcast([128, 8, 128]), ALU.mult, [TMP1.b, cst.b], [TMP1.b])
        for h in range(8):
            tr(PW.t[:, h * 128:(h + 1) * 128], TMP1.t[:, h * 128:(h + 1) * 128], ident, [TMP1.b, cst.b], [PW.b])
        cp("dve", wsT.t[:], PW.t[:], [PW.b], [wsT.b])
        for h in range(8):
            mm(PO.t[0:8, h * 128:(h + 1) * 128], TMP1.t[0:8, h * 128:h * 128 + 8], cst.t[0:8, C_R:C_R + 128],
               True, True, [TMP1.b, cst.b], [PO.b])
        cp("dve", Xs.t[0:8, :], PO.t[0:8, :], [PO.b], [Xs.b])
        for h in range(8):
            mm(PW.t[:, h * 128:(h + 1) * 128], cst.t[0:8, C_R:C_R + 128], Xs.t[0:8, h * 128:(h + 1) * 128],
               True, True, [Xs.b, cst.b], [PW.b])
        tt("dve", v3(wsS.t[:], 8), v3(PW.t[:], 8),
           cst.t[:, C_BD:C_BD + 128].unsqueeze(1).to_broadcast([128, 8, 128]), ALU.mult, [PW.b, cst.b], [wsS.b])
        dma("sp", stg.t[0:1, 0:D], b_s.rearrange("h t -> (h t)").unsqueeze(0), [], [stg.b], "bs1")
        dma("sp", stg.t[32:33, 0:D].rearrange("p (h s t) -> p h s t", h=8, s=16),
            b_s[:, 0:8].unsqueeze(0).unsqueeze(2).to_broadcast([1, 8, 16, 8]), [], [stg.b], "bsS")
        P.op("dve", lambda e: e.memset(stg.t[0:1, D:D + 256], 1.0), [], [stg.b])
        P.op("dve", lambda e: e.memset(stg.t[32:33, D:D + 256], 1.0), [], [stg.b])
        for br_, bo_ in ((0, 0), (32, 8)):
            for hh_ in range(8):
                mm(PX.t[:, bo_ + hh_:bo_ + hh_ + 1], stg.t[br_:br_ + 1, hh_ * 128:(hh_ + 1) * 128], stg.t[br_:br_ + 1, D:D + 1],
                   True, True, [stg.b], [PX.b])
        cp("dve", bsT.t[:], PX.t[:, 0:16], [PX.b], [bsT.b])

        class Task:
            def __init__(self, gen):
                self.gen = gen
                self.done = False

        tasks = []

        def spawn(gen):
            t = Task(gen)
            tasks.append(t)
            return t

        def run_all():
            while tasks:
                for t in list(tasks):
                    try:
                        next(t.gen)
                    except StopIteration:
                        t.done = True
                        tasks.remove(t)

        def join(tl):
            while not all(t.done for t in tl):
                yield

        _A = [("glu", 2), ("glu", 3), ("val", 0), ("val", 1), ("gate", 4), ("gate", 5)]
        _B = [("f", 8), ("f", 9), ("q", 6), ("q", 7), ("i", 10), ("i", 11), ("bg", 12), ("bg", 13)]
        L0_ORDER = _A + _B if L0_AB else _B + _A
        L1_ORDER = [("v", g) for g in range(4, 8)] + [("gate", g) for g in range(8, 12)] + [("u", g) for g in range(0, 4)]
        wq = []
        n_super = SEQ // (128 * NSUB) + 1
        for _ in range(n_super):
            wq += [(w_in_ab, g * 512, 512, 8) for _, g in L0_ORDER]
            wq += [(w_out_ab, g * 256, 256, 16) for g in range(4)]
            wq += [(w_in_c, g * 512, 512, 8) for _, g in L1_ORDER]
            wq += [(w_out_c, g * 256, 256, 16) for g in range(4)]
        wst = {"prod": 0, "cons": 0}

        scr = {}
        scrB = {}
        for nm, src in (("wb_in_ab", w_in_ab), ("wb_out_ab", w_out_ab), ("wb_in_c", w_in_c), ("wb_out_c", w_out_c)):
            scr[id(src)] = nc.dram_tensor(nm, list(src.shape), BF16, kind="Internal").ap()

        def w_fill():
            while wst["prod"] < len(wq) and wst["prod"] < wst["cons"] + NSLOT:
                n = wst["prod"]
                src, c0, ncols, nk = wq[n]
                slot = wslot[n % NSLOT]
                view = slot.t[:, 0:nk * ncols].rearrange("p (k n) -> p k n", k=nk)
                key = (id(src), c0)
                if key not in scrB:
                    scrB[key] = Buf("scr%d_%d" % (len(scrB), c0))
                    dma("pool", view, src[:, c0:c0 + ncols].rearrange("(k p) n -> p k n", p=128), [], [slot.b],
                        "w%d" % (n % NSLOT))
                    dma("sp", scr[id(src)][:, c0:c0 + ncols].rearrange("(k p) n -> p k n", p=128), view, [slot.b], [scrB[key]],
                        "cv%d" % len(scrB))
                else:
                    dma(W_QUEUE, view, scr[id(src)][:, c0:c0 + ncols].rearrange("(k p) n -> p k n", p=128),
                        [scrB[key]], [slot.b], ("w%d" if W_QUEUE == "pool" else "v%d") % (n % NSLOT))
                wst["prod"] += 1

        def w_get(src, c0):
            n = wst["cons"]
            assert wq[n][0] is src and wq[n][1] == c0, (n, c0)
            w_fill()
            _, _, ncols, nk = wq[n]
            slot = wslot[n % NSLOT]
            return slot, slot.t[:, 0:nk * ncols].rearrange("p (k n) -> p k n", k=nk)

        def w_done():
            wst["cons"] += 1
            w_fill()

        pzc = {"n": 0}

        def next_pz():
            pzc["n"] += 1
            return PZ[pzc["n"] % 2]

        def rms_stats(j, src):
            P.op("dve", lambda e: e.memset(rt[j].t[:, 0:1], 0.0), [], [rt[j].b])
            act(xnb[j].t[:], src.t[:], AF.Square, [src.b, rt[j].b], [xnb[j].b, rt[j].b], accum_out=rt[j].t[:, 0:1])
            ts("dve", rt[j].t[:, 2:3], rt[j].t[:, 0:1], 1.0 / D, EPS, ALU.mult, ALU.add, [rt[j].b], [rt[j].b])
            rsqrt(rt[j].t[:, 1:2], rt[j].t[:, 2:3], [rt[j].b], [rt[j].b])

        def t_norm(j, gcol, xsrc=None, r0=0):
            if xsrc is not None:
                dma("sp", xs[j].t[:], xsrc[r0:r0 + 128, :], [], [xs[j].b], "xs%d" % j)
            rms_stats(j, xs[j])
            yield
            tsm("dve", xnb[j].t[:], xs[j].t[:], rt[j].t[:, 1:2], [xs[j].b, rt[j].b], [xnb[j].b])
            yield
            for kc in range(8):
                tr(PT.t[:, kc * 128:(kc + 1) * 128], xnb[j].t[:, kc * 128:(kc + 1) * 128], identb.t[:],
                   [xnb[j].b, identb.b], [PT.b])
            tt("dve", v3(xnT[j].t[:], 8), v3(PT.t[:], 8), cparv[:, :, gcol:gcol + 1].to_broadcast([128, 8, 128]),
               ALU.mult, [PT.b, cpar.b], [xnT[j].b])
            yield

        def proj(j, wview, nk, lhs, ncols, lhs_buf, slot):
            pz = next_pz()
            lv = v3(lhs.t[:, 0:nk * 128], nk)
            for kc in range(nk):
                mm(pz.t[:, 0:ncols], lv[:, kc, :], wview[:, kc, :], kc == 0, kc == nk - 1, [lhs_buf, slot.b], [pz.b])
            return pz

        def transpose8(src_ap, src_buf):
            for kc in range(8):
                tr(PT.t[:, kc * 128:(kc + 1) * 128], src_ap[:, kc * 128:(kc + 1) * 128], identb.t[:],
                   [src_buf, identb.b], [PT.b])

        def t_out_proj(nsub, wsrc):
            for g in range(4):
                slot, wv = w_get(wsrc, g * 256)
                for j in range(nsub):
                    pz = proj(j, wv, 16, cat[j], 256, cat[j].b, slot)
                    tt("dve", xs[j].t[:, g * 256:(g + 1) * 256], xs[j].t[:, g * 256:(g + 1) * 256], pz.t[:, 0:256],
                       ALU.add, [xs[j].b, pz.b], [xs[j].b])
                    yield
                w_done()

        def supertile(kind, nsub, tok0, is_last_prompt):
            sample = kind == "s"
            xsrc = xsm if sample else xp
            L = 8 if sample else 128 * nsub
            Lt = 128 * nsub
            nseg = 16 if sample else 1
            segw = 38 if sample else 30 + L
            cbv = cb.t[:, 0:8 * nseg * segw].rearrange("p (c s l) -> p c s l", c=8, s=nseg)
            UC, MK, UX = (C_UCS, C_MKS, C_UXS) if sample else (C_UCP, C_MKP, C_UXP)
            nch = 16 if sample else 2
            nc3 = 3 * nch
            chunks = [(8 * c, 8 * c + 8) for c in range(16)] if sample else [(0, 64), (64, 128)]

            def t_hist():
                for rt4 in range(4):
                    dma("sp", TMP2.t[0:120, :], sconv[rt4 * 120:(rt4 + 1) * 120, :], [], [TMP2.b], "sch")
                    for ch in range(8):
                        tr(PW.t[:, ch * 128:ch * 128 + 120], TMP2.t[0:120, ch * 128:(ch + 1) * 128],
                           cst.t[0:120, C_ID:C_ID + 120], [TMP2.b, cst.b], [PW.b])
                    cp("dve", cbv[:, :, rt4 * 4:(rt4 + 1) * 4, 0:30],
                       v3(PW.t[:], 8)[:, :, 0:120].rearrange("p c (s l) -> p c s l", s=4), [PW.b], [cb.b])
                    yield
                dma("sp", conv_s[:, 0:22, :], sconv.rearrange("(s r) c -> s r c", r=30)[:, 8:30, :], [], [],
                    "cs0", is_output=True)

            def t_hgrn():
                for j in range(nsub):
                    kt = KQ[j].t[:, 0:1024]; qt = KQ[j].t[:, 1024:2048]
                    vv = VS[j].t[:, 0:1024]; sbg = VS[j].t[:, 1024:2048]
                    sm = SMG[j].t[:, 0:1024]
                    qT = QKT[j].t[:, 0:1024]; kT = QKT[j].t[:, 1024:2048]
                    transpose8(qt, KQ[j].b)
                    cp("act", qT, PT.t[:], [PT.b], [QKT[j].b])
                    yield
                    transpose8(kt, KQ[j].b)
                    cp("act", kT, PT.t[:], [PT.b], [QKT[j].b])
                    yield
                    for h in range(8):
                        hs = slice(h * 128, (h + 1) * 128)
                        mm(PW.t[:, hs], kT[:, hs], qT[:, hs], True, True, [QKT[j].b], [PW.b])
                    tt("dve", v3(sm, 8), v3(PW.t[:], 8), cst.t[:, MK:MK + 128].unsqueeze(1).to_broadcast([128, 8, 128]),
                       ALU.mult, [PW.b, cst.b], [SMG[j].b])
                    yield
                    for h in range(8):
                        hs = slice(h * 128, (h + 1) * 128)
                        mm(PO.t[:, hs], vv[:, hs], sm[:, hs], h % 4 == 0, False, [VS[j].b, SMG[j].b], [PO.b])
                    yield
                    scv = v3(scal[j].t[:, 0:8 * nc3], 8)
                    for ci, (c0, c1) in enumerate(chunks):
                        if sample:
                            Ss = Sld[ci % 2]
                            dma("sp", v3(Ss.t[:], 8), shg[ci].rearrange("h d v -> d h v"), [], [Ss.b], "sld%d" % (ci % 2))
                        else:
                            Ss = Sst
                        bcol = lambda k: scv[:, :, 3 * ci + k:3 * ci + k + 1].to_broadcast([128, 8, 128])
                        tt("dve", v3(Sp.t[:], 8), v3(Ss.t[:], 8), bcol(0), ALU.mult, [Ss.b, scal[j].b], [Sp.b])
                        if sample:
                            tsm("dve", VM.t[:], vv, cst.t[:, C_CMS + ci:C_CMS + ci + 1], [VS[j].b, cst.b], [VM.b])
                        yield
                        for h in range(8):
                            hs = slice(h * 128, (h + 1) * 128)
                            if sample:
                                mm(PW.t[:, hs], kt[:, hs], VM.t[:, hs], True, True, [KQ[j].b, VM.b], [PW.b])
                            else:
                                mm(PW.t[:, hs], kt[c0:c1, hs], vv[c0:c1, hs], True, True, [KQ[j].b, VS[j].b], [PW.b])
                        for h in range(8):
                            hs = slice(h * 128, (h + 1) * 128)
                            mm(PO.t[:, h * 128 + c0:h * 128 + c1], Sp.t[:, hs], qT[:, h * 128 + c0:h * 128 + c1],
                               False, (ci == len(chunks) - 1) and (h % 4 == 3), [Sp.b, QKT[j].b], [PO.b])
                        tt("dve", v3(TMP1.t[:], 8), v3(PW.t[:], 8), bcol(1), ALU.mult, [PW.b, scal[j].b], [TMP1.b])
                        tt("dve", v3(Ss.t[:], 8), v3(Ss.t[:], 8), bcol(2), ALU.mult, [Ss.b, scal[j].b], [Ss.b])
                        tt("dve", Ss.t[:], Ss.t[:], TMP1.t[:], ALU.add, [Ss.b, TMP1.b], [Ss.b])
                        if sample:
                            dma("sp", hg_s[ci].rearrange("h d v -> d h v"), v3(Ss.t[:], 8), [Ss.b], [],
                                "sst%d" % (ci % 2), is_output=True)
                        yield
                    if is_last_prompt and j == nsub - 1:
                        dma("sp", hg_p.rearrange("h d v -> d h v"), v3(Sst.t[:], 8), [Sst.b], [], "hgp", is_output=True)
                    act(osq.t[:], PO.t[:], AF.Square, [PO.b], [osq.b])
                    yield
                    for h2 in range(2):
                        mm(PW.t[:, h2 * 512:(h2 + 1) * 512], onesb.t[:], osq.t[:, h2 * 512:(h2 + 1) * 512], True, True,
                           [onesb.b, osq.b], [PW.b])
                    ts("dve", TMP1.t[:], PW.t[:], 1.0 / 128, EPS, ALU.mult, ALU.add, [PW.b], [TMP1.b])
                    rsqrt(TMP1.t[:], TMP1.t[:], [TMP1.b], [TMP1.b])
                    yield
                    transpose8(sbg, VS[j].b)
                    tt("dve", v3(GS.t[:], 8), v3(PT.t[:], 8), gon.t[:, 0:8].unsqueeze(2).to_broadcast([128, 8, 128]), ALU.mult,
                       [PT.b, gon.b], [GS.b])
                    yield
                    tt("dve", TMP2.t[:], PO.t[:], TMP1.t[:], ALU.mult, [PO.b, TMP1.b], [TMP2.b])
                    tt(CAT_ENG, cat[j].t[:, 1024:2048], TMP2.t[:], GS.t[:], ALU.mult, [TMP2.b, GS.b], [cat[j].b])
                    yield

            def t_conv():
                for j in range(nsub):
                    for ch in range(8):
                        tr(PX.t[:, (ch % 4) * 128:(ch % 4 + 1) * 128], A[j].t[:, ch * 128:(ch + 1) * 128], ident,
                           [A[j].b, cst.b], [PX.b])
                        if ch % 4 == 3:
                            c4 = ch // 4
                            if sample:
                                cp("act", cbv[:, 4 * c4:4 * c4 + 4, :, 30:38],
                                   PX.t[:].rearrange("p (c s l) -> p c s l", c=4, s=16), [PX.b], [cb.b])
                            else:
                                cp("act", cbv[:, 4 * c4:4 * c4 + 4, 0, 30 + j * 128:30 + (j + 1) * 128], v3(PX.t[:], 4),
                                   [PX.b], [cb.b])
                            yield
                    if sample:
                        for sq in range(16):
                            dma("sp", conv_s[sq, 22:30, :], A[j].t[8 * sq:8 * sq + 8, :], [A[j].b], [], "cs1", is_output=True)
                    elif is_last_prompt and j == nsub - 1:
                        dma("sp", conv_p, A[j].t[98:128, :], [A[j].b], [], "cvp", is_output=True)
                ybv = v3(yb.t[:, 0:8 * Lt], 8)

                def win(ch, k):
                    return cbv[:, ch, :, k:k + L] if sample else cbv[:, ch, 0, k:k + L]

                def accv(ch):
                    return ybv[:, ch, :].rearrange("p (s l) -> p s l", s=16) if sample else ybv[:, ch, :]
                ysqv = v3(ysq.t[:, 0:8 * Lt], 8)
                for ch in range(8):
                    db = dgb[ch % 2]
                    dB = dgB[ch % 2]
                    dv = db.t[:].rearrange("p (k m) -> p k m", k=31)
                    for q, (k0, k1, eng) in enumerate(((0, DG_SPLIT, "dve"), (DG_SPLIT, CONV_PE_TAPS, "pool"))):
                        tt(eng, dv[:, k0:k1, :], ident.unsqueeze(1).to_broadcast([128, k1 - k0, 128]),
                           cparv[:, ch, k0:k1].unsqueeze(2).to_broadcast([128, k1 - k0, 128]), ALU.mult,
                           [cst.b, cpar.b], [dB[q]])
                    yield
                    NP = CONV_PE_TAPS
                    for k in range(NP):
                        if sample:
                            mm(PX.t[:, 0:Lt].rearrange("p (s l) -> p s l", s=16), dv[:, k, :], cbv[:, ch, :, k:k + L],
                               k == 0, k == NP - 1, [dB[0], dB[1], cb.b], [PX.b])
                        else:
                            mm(PX.t[:, 0:Lt], dv[:, k, :], cbv[:, ch, 0, k:k + L], k == 0, k == NP - 1, [dB[0], dB[1], cb.b], [PX.b])
                    bias = cparv[:, ch, 31:32]
                    act(ybv[:, ch, :], PX.t[:, 0:Lt], AF.Identity, [PX.b, cpar.b], [ybB[ch]], bias=bias)
                    if NP == 31:
                        act(ysqv[:, ch, :], PX.t[:, 0:Lt], AF.Square, [PX.b, cpar.b], [ysq.b], bias=bias)
                    yield
                    for k in range(NP, 31):
                        av = ybv[:, ch, :].rearrange("p (s l) -> p s l", s=16) if sample else ybv[:, ch, :]
                        wv_ = cbv[:, ch, :, k:k + L] if sample else cbv[:, ch, 0, k:k + L]
                        P.op("dve", (lambda av, wv_, ch, k: lambda e: e.scalar_tensor_tensor(
                            out=av, in0=wv_, scalar=cparv[:, ch, k:k + 1], in1=av, op0=ALU.mult, op1=ALU.add))(av, wv_, ch, k),
                             [cb.b, cpar.b, ybB[ch]], [ybB[ch]], cost=fsz(av) / 0.96 + 90)
                        if (k - NP) % 4 == 3:
                            yield
                    if NP < 31:
                        act(ysqv[:, ch, :], ybv[:, ch, :], AF.Square, [ybB[ch]], [ysq.b])
                        yield
                yield
                for ch in range(8):
                    mm(PW.t[:, 0:Lt], onesmf.t[:], ybv[:, ch, :], ch == 0, ch == 7, [onesmf.b, ybB[ch]], [PW.b])
                for ch in range(8):
                    mm(PW.t[:, 512:512 + Lt], onesm.t[:], v3(ysq.t[:, 0:8 * Lt], 8)[:, ch, :], ch == 0, ch == 7,
                       [onesm.b, ysq.b], [PW.b])
                cp("act", MEAN.t[:, 0:Lt], PW.t[:, 0:Lt], [PW.b], [MEAN.b])
                tt("dve", VAR.t[:, 0:Lt], MEAN.t[:, 0:Lt], MEAN.t[:, 0:Lt], ALU.mult, [MEAN.b], [VAR.b])
                tt("dve", VAR.t[:, 0:Lt], PW.t[:, 512:512 + Lt], VAR.t[:, 0:Lt], ALU.subtract, [PW.b, VAR.b], [VAR.b])
                ts("dve", VAR.t[:, 0:Lt], VAR.t[:, 0:Lt], 1.0, EPS, ALU.mult, ALU.add, [VAR.b], [VAR.b])
                rsqrt(RSTD.t[:, 0:Lt], VAR.t[:, 0:Lt], [VAR.b], [RSTD.b])
                yield
                tt("dve", ybv, ybv, MEAN.t[:, 0:Lt].unsqueeze(1).to_broadcast([128, 8, Lt]), ALU.subtract,
                   ybB + [MEAN.b], ybB)
                tt("dve", ybv, ybv, RSTD.t[:, 0:Lt].unsqueeze(1).to_broadcast([128, 8, Lt]), ALU.mult, ybB + [RSTD.b], ybB)
                yield
                for ch in range(8):
                    act(ybv[:, ch, :], ybv[:, ch, :], AF.Silu, [ybB[ch], cpar.b], [ybB[ch]],
                        scale=cparv[:, ch, 32:33], bias=cparv[:, ch, 33:34])
                yield
                for j in range(nsub):
                    transpose8(SMG[j].t[:, 1024:2048], SMG[j].b)
                    tt("dve", v3(cat[j].t[:, 0:1024], 8), ybv[:, :, j * 128:(j + 1) * 128], v3(PT.t[:], 8), ALU.mult,
                       ybB + [PT.b], [cat[j].b])
                    yield
                if not sample:
                    cp("act", cbv[:, :, 0, 0:30], cbv[:, :, 0, L:L + 30], [cb.b], [cb.b])

            def t_vln(j):
                VG = FE[j]
                r = rt[j]
                act(KQ[j].t[:], VG.t[:], AF.Square, [VG.b, r.b], [KQ[j].b, r.b], accum_out=r.t[:, 8:9])
                P.op("dve", lambda e: e.tensor_reduce(out=r.t[:, 9:10], in_=r.t[:, 4:8], axis=AX.X, op=ALU.add), [r.b], [r.b])
                ts("dve", r.t[:, 9:10], r.t[:, 9:10], 1.0 / 2048, 0.0, ALU.mult, ALU.add, [r.b], [r.b])
                tt("dve", r.t[:, 10:11], r.t[:, 9:10], r.t[:, 9:10], ALU.mult, [r.b], [r.b])
                ts("dve", r.t[:, 11:12], r.t[:, 8:9], 1.0 / 2048, EPS, ALU.mult, ALU.add, [r.b], [r.b])
                tt("dve", r.t[:, 11:12], r.t[:, 11:12], r.t[:, 10:11], ALU.subtract, [r.b], [r.b])
                rsqrt(r.t[:, 12:13], r.t[:, 11:12], [r.b], [r.b])
                tt("dve", r.t[:, 13:14], r.t[:, 9:10], r.t[:, 12:13], ALU.mult, [r.b], [r.b])
                ts("dve", r.t[:, 13:14], r.t[:, 13:14], -1.0, 0.0, ALU.mult, ALU.add, [r.b], [r.b])
                yield
                act(VG.t[:], VG.t[:], AF.Identity, [VG.b, r.b], [VG.b], scale=r.t[:, 12:13], bias=r.t[:, 13:14])
                yield
                if VLN_SPLIT:
                    for eng_, sl_ in (("pool", slice(0, VLN_SPLIT)), ("dve", slice(VLN_SPLIT, 2048))):
                        tt(eng_, VG.t[:, sl_], VG.t[:, sl_], lncg_bc.t[:, sl_], ALU.mult, [VG.b, lncg_bc.b], [VG.b])
                    yield
                    for eng_, sl_ in (("pool", slice(0, VLN_SPLIT)), ("dve", slice(VLN_SPLIT, 2048))):
                        tt(eng_, VG.t[:, sl_], VG.t[:, sl_], lncb_bc.t[:, sl_], ALU.add, [VG.b, lncb_bc.b], [VG.b])
                else:
                    tt("pool", VG.t[:], VG.t[:], lncg_bc.t[:], ALU.mult, [VG.b, lncg_bc.b], [VG.b])
                    yield
                    tt("dve", VG.t[:], VG.t[:], lncb_bc.t[:], ALU.add, [VG.b, lncb_bc.b], [VG.b])
                yield
                cp("act", VS[j].t[:], VG.t[:], [VG.b], [VS[j].b])
                if sample:
                    dma("sp", gv_s, VG.t[:], [VG.b], [], "gvs", is_output=True)
                elif is_last_prompt and j == nsub - 1:
                    dma("sp", gv_p, VG.t[:], [VG.b], [], "gvp", is_output=True)

            def t_mix():
                wsX = wsS if sample else wsT
                br = 32 if sample else 0
                for j in range(nsub):
                    for half in range(2):
                        PH = PW if half == 0 else PO
                        for hl in range(4):
                            h = half * 4 + hl
                            mm(PH.t[:, hl * 256:(hl + 1) * 256], wsX.t[:, h * 128:(h + 1) * 128], VS[j].t[:, h * 256:(h + 1) * 256],
                               True, True, [wsX.b, VS[j].b], [PH.b])
                        bo = 8 if sample else 0
                        for hl in range(4):
                            h = half * 4 + hl
                            P.op("dve", (lambda j, h, hl, PH, bo: lambda e: e.scalar_tensor_tensor(
                                out=KQ[j].t[:, h * 256:(h + 1) * 256], in0=PH.t[:, hl * 256:(hl + 1) * 256],
                                scalar=bsT.t[:, bo + h:bo + h + 1], in1=SMG[j].t[:, h * 256:(h + 1) * 256],
                                op0=ALU.add, op1=ALU.mult))(j, h, hl, PH, bo),
                                 [PH.b, SMG[j].b, bsT.b], [KQ[j].b], cost=360.0)
                        yield
                    for half in range(2):
                        transpose8(KQ[j].t[:, half * 1024:(half + 1) * 1024], KQ[j].b)
                        cp("act", cat[j].t[:, half * 1024:(half + 1) * 1024], PT.t[:], [PT.b], [cat[j].b])
                        yield

            tl = [spawn(t_norm(j, 34, xsrc, tok0 + j * 128)) for j in range(nsub)]
            if sample:
                tl.append(spawn(t_hist()))
            yield from join(tl)
            mix_tasks = []
            for kindg, g in L0_ORDER:
                hh = g % 2
                cs = slice(hh * 512, (hh + 1) * 512)
                slot, wv = w_get(w_in_ab, g * 512)
                for j in range(nsub):
                    pz = proj(j, wv, 8, xnT[j], 512, xnT[j].b, slot)
                    F = FE[j].t[:, 0:1024]; E1 = FE[j].t[:, 1024:2048]
                    if kindg == "f":
                        act(F[:, cs], pz.t[:], AF.Sigmoid, [pz.b], [FE[j].b])
                        if hh == 1:
                            yield
                            tt("dve", F, F, oml_bc.t[:], ALU.mult, [FE[j].b, oml_bc.b], [FE[j].b])
                            tt("dve", F, F, lb_bc.t[:], ALU.add, [FE[j].b, lb_bc.b], [FE[j].b])
                            ts("dve", kk[j].t[:], F, -1.0, 1.0, ALU.mult, ALU.add, [FE[j].b], [kk[j].b])
                            act(F, F, AF.Ln, [FE[j].b], [FE[j].b])
                            yield
                            for h2 in range(2):
                                mm(PW.t[:, h2 * 512:(h2 + 1) * 512], cst.t[:, UC:UC + 128], F[:, h2 * 512:(h2 + 1) * 512],
                                   True, True, [cst.b, FE[j].b], [PW.b])
                            for h in range(8):
                                mm(PX.t[:, h * nc3:(h + 1) * nc3], F[:, h * 128:(h + 1) * 128], cst.t[:, UX:UX + nc3],
                                   True, True, [cst.b, FE[j].b], [PX.b])
                            act(E1, PW.t[:], AF.Exp, [PW.b], [FE[j].b])
                            act(TMP1.t[:], PW.t[:], AF.Exp, [PW.b], [TMP1.b], scale=-1.0)
                            act(scal[j].t[:, 0:8 * nc3], PX.t[:, 0:8 * nc3], AF.Exp, [PX.b], [scal[j].b])
                            tt("dve", KQ[j].t[:, 0:1024], kk[j].t[:], TMP1.t[:], ALU.mult, [kk[j].b, TMP1.b], [KQ[j].b])
                    elif kindg == "q":
                        tt("dve", KQ[j].t[:, 1024 + hh * 512:1024 + (hh + 1) * 512], pz.t[:], E1[:, cs], ALU.mult,
                           [pz.b, FE[j].b], [KQ[j].b])
                    elif kindg == "i":
                        act(VS[j].t[:, cs], pz.t[:], AF.Silu, [pz.b], [VS[j].b])
                    elif kindg == "bg":
                        act(VS[j].t[:, 1024 + hh * 512:1024 + (hh + 1) * 512], pz.t[:], AF.Silu, [pz.b], [VS[j].b])
                    elif kindg == "glu":
                        act(A[j].t[:, cs], pz.t[:], AF.Sigmoid, [pz.b], [A[j].b])
                    elif kindg == "val":
                        tt("dve", A[j].t[:, cs], pz.t[:], A[j].t[:, cs], ALU.mult, [pz.b, A[j].b], [A[j].b])
                    elif kindg == "gate":
                        act(SMG[j].t[:, 1024 + hh * 512:1024 + (hh + 1) * 512], pz.t[:], AF.Silu, [pz.b], [SMG[j].b])
                    yield
                w_done()
                if (kindg, g) == ("gate", 5):
                    mix_tasks.append(spawn(t_conv()))
                if (kindg, g) == ("bg", 13):
                    mix_tasks.append(spawn(t_hgrn()))
            yield from join(mix_tasks)
            yield from t_out_proj(nsub, w_out_ab)
            yield from join([spawn(t_norm(j, 35)) for j in range(nsub)])
            vtasks = []
            for kindg, g in L1_ORDER:
                gi = g % 4
                cs = slice(gi * 512, (gi + 1) * 512)
                slot, wv = w_get(w_in_c, g * 512)
                for j in range(nsub):
                    pz = proj(j, wv, 8, xnT[j], 512, xnT[j].b, slot)
                    VG = FE[j]; SG = QKT[j]; UG = SMG[j]
                    if kindg == "v":
                        if gi == 0:
                            P.op("dve", lambda e, j=j: e.memset(rt[j].t[:, 4:12], 0.0), [], [rt[j].b])
                        act(VG.t[:, cs], pz.t[:], AF.Gelu_apprx_tanh, [pz.b, rt[j].b], [VG.b, rt[j].b],
                            accum_out=rt[j].t[:, 4 + gi:5 + gi])
                        if gi == 3:
                            vtasks.append(spawn(t_vln(j)))
                    elif kindg == "gate":
                        act(SG.t[:, cs], pz.t[:], AF.Silu, [pz.b], [SG.b])
                    else:
                        act(UG.t[:, cs], pz.t[:], AF.Gelu_apprx_tanh, [pz.b], [UG.b])
                        if gi == 3:
                            tt("dve", UG.t[:], UG.t[:], SG.t[:], ALU.mult, [UG.b, SG.b], [UG.b])
                    yield
                w_done()
            yield from join(vtasks)
            yield from t_mix()
            yield from t_out_proj(nsub, w_out_c)
            for j in range(nsub):
                rms_stats(j, xs[j])
                P.op("dve", lambda e, j=j: e.scalar_tensor_tensor(out=TMP2.t[:], in0=xs[j].t[:], scalar=rt[j].t[:, 1:2],
                                                                  in1=fin_bc.t[:], op0=ALU.mult, op1=ALU.mult),
                     [xs[j].b, rt[j].b, fin_bc.b], [TMP2.b])
                r0 = tok0 + j * 128
                dst = y_s if sample else y_p
                dma("sp", dst[r0:r0 + 128, :], TMP2.t[:], [TMP2.b], [], "yout", is_output=True)
                yield

        def main():
            nst = SEQ // (128 * NSUB)
            for s_i in range(nst):
                yield from supertile("p", NSUB, s_i * 128 * NSUB, s_i == nst - 1)
            yield from supertile("s", 1, 0, False)

        w_fill()
        spawn(main())
        run_all()
        assert wst["cons"] == len(wq), (wst, len(wq))
        if LIST_SCHED:
            P.schedule(SCHED_WINDOW)
        P.emit(nc, st)
    return nc


_CACHE = {}


def kernel(**inputs):
    f = lambda a: np.ascontiguousarray(np.asarray(a, dtype=np.float32))
    x_prompt = f(inputs["x_prompt"]); x_sample = f(inputs["x_sample"])
    state_conv = f(inputs["state_conv"]); state_hgrn = f(inputs["state_hgrn"])
    if "nc" not in _CACHE:
        _CACHE["nc"] = build_program()
    nc = _CACHE["nc"]
    shared = {
        "norm_ab": f(inputs["norm_ab"]).reshape(1, D), "w_in_ab": f(inputs["w_in_ab"])[0],
        "conv_w": f(inputs["conv_w"])[0], "conv_b": f(inputs["conv_b"]).reshape(1, D),
        "ln_a_g": f(inputs["ln_a_g"]).reshape(1, D), "ln_a_b": f(inputs["ln_a_b"]).reshape(1, D),
        "lb_logits": f(inputs["lb_logits"]), "onorm_b": f(inputs["onorm_b"])[0],
        "w_out_ab": f(inputs["w_out_ab"])[0], "norm_c": f(inputs["norm_c"]).reshape(1, D),
        "w_in_c": f(inputs["w_in_c"])[0], "ln_c_g": f(inputs["ln_c_g"]).reshape(1, 2048),
        "ln_c_b": f(inputs["ln_c_b"]).reshape(1, 2048), "w_s": f(inputs["w_s"])[0], "b_s": f(inputs["b_s"])[0],
        "w_out_c": f(inputs["w_out_c"])[0], "final_norm": f(inputs["final_norm"]).reshape(1, D),
        "cst": make_consts(),
    }
    in_maps = []
    for c in range(NCORES):
        m = dict(shared)
        m["xp"] = x_prompt[c]
        m["xsm"] = x_sample[16 * c:16 * (c + 1)].reshape(128, D)
        m["sconv"] = state_conv[0, 16 * c:16 * (c + 1)].reshape(480, D)
        m["shg"] = state_hgrn[0, 16 * c:16 * (c + 1)]
        in_maps.append(m)
    res = run_bass_kernel_spmd(nc, in_maps, core_ids=list(range(NCORES)))
    R = res.results
    y_prompt = np.stack([R[c]["y_p"] for c in range(NCORES)], 0)
    y_sample = np.concatenate([R[c]["y_s"].reshape(16, 8, D) for c in range(NCORES)], 0)
    conv_prompt = np.stack([R[c]["conv_p"] for c in range(NCORES)], 0)[None]
    hgrn_prompt = np.stack([R[c]["hg_p"] for c in range(NCORES)], 0)[None]
    gv_prompt = np.stack([R[c]["gv_p"] for c in range(NCORES)], 0)[None]
    conv_sample = np.concatenate([R[c]["conv_s"] for c in range(NCORES)], 0)[None]
    hgrn_sample = np.concatenate([R[c]["hg_s"] for c in range(NCORES)], 0)[None]
    gv_sample = np.concatenate([R[c]["gv_s"].reshape(16, 8, 2048) for c in range(NCORES)], 0)[None]
    return tuple(np.ascontiguousarray(a, dtype=np.float32) for a in
                 (y_prompt, y_sample, conv_prompt, hgrn_prompt, gv_prompt, conv_sample, hgrn_sample, gv_sample))
```
